# Optimizing a Trainium2 kernel written in Bass

```python
import math
import jax, jax.numpy as jnp
from jax import lax
import numpy as np

D_MODEL = 1024
BATCH = 16
SEQ = 2048
DEPTH = 1

HEAD_DIM = 64
MOBA_HEADS = 8
MOBA_WIDTH = MOBA_HEADS * HEAD_DIM
MOBA_BLOCK = 256
MOBA_TOPK = 3
MOBA_QCHUNK = 32
MLA_HEADS = 8
MLA_Q_RANK = 256
MLA_KV_RANK = 128
MLA_NOPE_DIM = 64
MLA_ROPE_DIM = 32
MLA_V_DIM = 64
MLA_QK_DIM = MLA_NOPE_DIM + MLA_ROPE_DIM
MLA_WIDTH = MLA_HEADS * MLA_V_DIM
MIX_WIDTH = MOBA_WIDTH + MLA_WIDTH
IN_WIDTH = 3 * MOBA_WIDTH + MLA_Q_RANK + MLA_KV_RANK + MLA_ROPE_DIM
ATTN_QBLOCK = 128
D_FF = -(-8 * D_MODEL // (3 * 256)) * 256
ROPE_THETA = 10000.0
EPS = 1e-6

kernel_name = "hybrid_moba_mla_parallel_heads"


def rms_norm(x, g):
    xf = x.astype(jnp.float32)
    y = xf * lax.rsqrt(jnp.mean(xf * xf, axis=-1, keepdims=True) + EPS)
    return (y * g.astype(jnp.float32)).astype(x.dtype)


def rope_tables(seq, dim):
    inv = ROPE_THETA ** (-jnp.arange(0, dim, 2, dtype=jnp.float32) / dim)
    ang = jnp.arange(seq, dtype=jnp.float32)[:, None] * inv[None, :]
    return jnp.cos(ang), jnp.sin(ang)


def apply_rope(x, cos, sin):
    half = x.shape[-1] // 2
    xf = x.astype(jnp.float32)
    x1, x2 = xf[..., :half], xf[..., half:]
    c, s = cos[:, None, :], sin[:, None, :]
    return jnp.concatenate([x1 * c - x2 * s, x2 * c + x1 * s], axis=-1).astype(x.dtype)


def moba_attention(q, k, v):
    B, S, H, D = q.shape
    f32 = jnp.float32
    nb = -(-S // MOBA_BLOCK)
    sp = nb * MOBA_BLOCK
    pad = sp - S
    q, k, v = [jnp.pad(t, ((0, 0), (0, pad), (0, 0), (0, 0))).transpose(0, 2, 1, 3)
               for t in (q, k, v)]
    scale = D ** -0.5
    blk = jnp.arange(sp) // MOBA_BLOCK
    qb = q.reshape(B, H, nb, MOBA_BLOCK, D)
    kb = k.reshape(B, H, nb, MOBA_BLOCK, D)
    vb = v.reshape(B, H, nb, MOBA_BLOCK, D)

    s_self = jnp.einsum('bhnqd,bhnkd->bhnqk', qb, kb, preferred_element_type=f32) * scale
    tri = jnp.tril(jnp.ones((MOBA_BLOCK, MOBA_BLOCK), dtype=bool))
    s_self = jnp.where(tri, s_self, -jnp.inf)
    m_self = jnp.max(s_self, axis=-1, keepdims=True)
    p_self = jnp.exp(s_self - m_self)
    l_self = jnp.sum(p_self, axis=-1, keepdims=True)
    o_self = jnp.einsum('bhnqk,bhnkd->bhnqd', p_self, vb.astype(f32)) / l_self
    lse_self = (m_self + jnp.log(l_self))[..., 0].reshape(B, H, sp)
    o_self = o_self.reshape(B, H, sp, D)

    n_sel = min(MOBA_TOPK, nb - 1)
    if n_sel == 0:
        o = o_self
    else:
        k_mean = jnp.mean(kb.astype(f32), axis=3)
        gate = jnp.einsum('bhsd,bhnd->bhsn', q.astype(f32), k_mean)
        past = jnp.arange(nb)[None, :] < blk[:, None]
        gate = jnp.where(past, gate, -jnp.inf)
        _, idx = lax.top_k(gate, n_sel)
        valid = jnp.arange(n_sel)[None, :] < blk[:, None]

        nc = sp // MOBA_QCHUNK

        def to_chunks(t):
            t = t.reshape(B, H, nc, MOBA_QCHUNK, *t.shape[3:])
            return jnp.moveaxis(t, 2, 0)

        bi = jnp.arange(B)[:, None, None, None]
        hi = jnp.arange(H)[None, :, None, None]

        def past_chunk(args):
            qc, ic, vc = args
            kg = kb[bi, hi, ic]
            vg = vb[bi, hi, ic]
            s = jnp.einsum('bhqd,bhqnjd->bhqnj', qc, kg, preferred_element_type=f32) * scale
            s = jnp.where(vc[None, None, :, :, None], s, -jnp.inf)
            m = jnp.max(s, axis=(-2, -1), keepdims=True)
            m = jnp.where(jnp.isfinite(m), m, 0.0)
            p = jnp.exp(s - m)
            l = jnp.sum(p, axis=(-2, -1))
            o = jnp.einsum('bhqnj,bhqnjd->bhqd', p, vg.astype(f32))
            o = o / jnp.where(l > 0, l, 1.0)[..., None]
            lse = m[..., 0, 0] + jnp.log(l)
            return o, lse

        o_past, lse_past = lax.map(
            past_chunk,
            (to_chunks(q), to_chunks(idx), valid.reshape(nc, MOBA_QCHUNK, n_sel)))
        o_past = jnp.moveaxis(o_past, 0, 2).reshape(B, H, sp, D)
        lse_past = jnp.moveaxis(lse_past, 0, 2).reshape(B, H, sp)
        m = jnp.maximum(lse_self, lse_past)
        w_s = jnp.exp(lse_self - m)
        w_p = jnp.exp(lse_past - m)
        o = (w_s[..., None] * o_self + w_p[..., None] * o_past) / (w_s + w_p)[..., None]
    return o[:, :, :S].transpose(0, 2, 1, 3).astype(v.dtype)


def causal_attention(q, k, v):
    B, S, H, Dk = q.shape
    Dv = v.shape[-1]
    scale = Dk ** -0.5
    nqb = S // ATTN_QBLOCK
    qb = q.reshape(B, nqb, ATTN_QBLOCK, H, Dk).transpose(1, 0, 3, 2, 4)
    kt = k.transpose(0, 2, 1, 3)
    vt = v.transpose(0, 2, 1, 3)
    kpos = jnp.arange(S)

    def block(args):
        qblk, start = args
        s = jnp.einsum('bhqd,bhkd->bhqk', qblk, kt, preferred_element_type=jnp.float32) * scale
        qpos = start + jnp.arange(ATTN_QBLOCK)
        s = jnp.where(kpos[None, :] <= qpos[:, None], s, -jnp.inf)
        p = jax.nn.softmax(s, axis=-1)
        return jnp.einsum('bhqk,bhkd->bhqd', p, vt.astype(jnp.float32))

    o = lax.map(block, (qb, jnp.arange(nqb) * ATTN_QBLOCK))
    return o.transpose(1, 0, 3, 2, 4).reshape(B, S, H, Dv).astype(v.dtype)


def mla_attention(c_q, c_kv, k_pe, q_a_g, w_q_up, kv_a_g, w_kv_up, q_norm_g, k_norm_g, cos, sin):
    B, S, _ = c_q.shape
    q = (rms_norm(c_q, q_a_g) @ w_q_up).reshape(B, S, MLA_HEADS, MLA_QK_DIM)
    kv = (rms_norm(c_kv, kv_a_g) @ w_kv_up).reshape(B, S, MLA_HEADS, MLA_NOPE_DIM + MLA_V_DIM)
    k_nope, v = kv[..., :MLA_NOPE_DIM], kv[..., MLA_NOPE_DIM:]
    k_rope = jnp.broadcast_to(k_pe[:, :, None, :], (B, S, MLA_HEADS, MLA_ROPE_DIM))
    k = jnp.concatenate([k_nope, k_rope], axis=-1)
    q = rms_norm(q, q_norm_g)
    k = rms_norm(k, k_norm_g)
    q = jnp.concatenate([q[..., :MLA_NOPE_DIM], apply_rope(q[..., MLA_NOPE_DIM:], cos, sin)], axis=-1)
    k = jnp.concatenate([k[..., :MLA_NOPE_DIM], apply_rope(k[..., MLA_NOPE_DIM:], cos, sin)], axis=-1)
    return causal_attention(q, k, v)


def setup_inputs(seed: int = 0) -> dict:
    key = jax.random.key(seed)
    ks = jax.random.split(key, 16)
    f32 = jnp.float32

    def w(k, shape, fan_in):
        return jax.random.normal(k, (DEPTH,) + shape, f32) * (fan_in ** -0.5)

    def g(k, n):
        return 1.0 + 0.02 * jax.random.normal(k, (DEPTH, n), f32)

    return {
        "x": jax.random.normal(ks[0], (BATCH, SEQ, D_MODEL), f32),
        "attn_norm_g": g(ks[1], D_MODEL),
        "w_in": w(ks[2], (D_MODEL, IN_WIDTH), D_MODEL),
        "moba_q_norm_g": g(ks[3], HEAD_DIM),
        "moba_k_norm_g": g(ks[4], HEAD_DIM),
        "mla_q_a_norm_g": g(ks[5], MLA_Q_RANK),
        "w_q_up": w(ks[6], (MLA_Q_RANK, MLA_HEADS * MLA_QK_DIM), MLA_Q_RANK),
        "mla_kv_a_norm_g": g(ks[7], MLA_KV_RANK),
        "w_kv_up": w(ks[8], (MLA_KV_RANK, MLA_HEADS * (MLA_NOPE_DIM + MLA_V_DIM)), MLA_KV_RANK),
        "mla_q_norm_g": g(ks[9], MLA_QK_DIM),
        "mla_k_norm_g": g(ks[10], MLA_QK_DIM),
        "w_o": w(ks[11], (MIX_WIDTH, D_MODEL), MIX_WIDTH),
        "ffn_norm_g": g(ks[12], D_MODEL),
        "w_gate": w(ks[13], (D_MODEL, D_FF), D_MODEL),
        "w_up": w(ks[14], (D_MODEL, D_FF), D_MODEL),
        "w_down": w(ks[15], (D_FF, D_MODEL), D_FF),
    }


def reference(x, attn_norm_g, w_in, moba_q_norm_g, moba_k_norm_g, mla_q_a_norm_g, w_q_up,
              mla_kv_a_norm_g, w_kv_up, mla_q_norm_g, mla_k_norm_g, w_o, ffn_norm_g,
              w_gate, w_up, w_down):
    B, S, _ = x.shape
    cos_a, sin_a = rope_tables(S, HEAD_DIM)
    cos_b, sin_b = rope_tables(S, MLA_ROPE_DIM)
    split_pts = [MOBA_WIDTH, 2 * MOBA_WIDTH, 3 * MOBA_WIDTH,
                 3 * MOBA_WIDTH + MLA_Q_RANK,
                 3 * MOBA_WIDTH + MLA_Q_RANK + MLA_KV_RANK]
    h = x
    for l in range(DEPTH):
        hn = rms_norm(h, attn_norm_g[l])
        proj = hn @ w_in[l]
        q_a, k_a, v_a, c_q, c_kv, k_pe = jnp.split(proj, split_pts, axis=-1)

        q_a = rms_norm(q_a.reshape(B, S, MOBA_HEADS, HEAD_DIM), moba_q_norm_g[l])
        k_a = rms_norm(k_a.reshape(B, S, MOBA_HEADS, HEAD_DIM), moba_k_norm_g[l])
        q_a = apply_rope(q_a, cos_a, sin_a)
        k_a = apply_rope(k_a, cos_a, sin_a)
        v_a = v_a.reshape(B, S, MOBA_HEADS, HEAD_DIM)
        o_a = moba_attention(q_a, k_a, v_a).reshape(B, S, MOBA_WIDTH)

        o_b = mla_attention(c_q, c_kv, k_pe, mla_q_a_norm_g[l], w_q_up[l], mla_kv_a_norm_g[l],
                            w_kv_up[l], mla_q_norm_g[l], mla_k_norm_g[l], cos_b, sin_b)
        o_b = o_b.reshape(B, S, MLA_WIDTH)

        h = h + jnp.concatenate([o_a, o_b], axis=-1) @ w_o[l]

        gn = rms_norm(h, ffn_norm_g[l])
        h = h + (jax.nn.silu(gn @ w_gate[l]) * (gn @ w_up[l])) @ w_down[l]
    return h.astype(x.dtype)
```

```python
import numpy as np
from contextlib import ExitStack
import concourse.bass as bass
import concourse.mybir as mybir
from concourse.bass_utils import run_bass_kernel_spmd
from concourse.ap import AP

F32 = mybir.dt.float32
BF16 = mybir.dt.bfloat16
ALU = mybir.AluOpType
AF = mybir.ActivationFunctionType
AX = mybir.AxisListType

NCORES = 8
SEQ = 2048
DM = 1024
NSEQ = 2
NT = 16
DFF = 2816
NFF = 22
EPS = 1e-6
NEG = -30000.0
TOK = NSEQ * SEQ


class Sched:
    def __init__(self, nc, es):
        self.nc = nc
        self.es = es
        self.E = {"pe": nc.tensor, "act": nc.scalar, "dve": nc.vector, "pool": nc.gpsimd, "sp": nc.sync}
        self.sem = {k: es.enter_context(nc.semaphore("s_" + k)) for k in self.E}
        self.cnt = {k: 0 for k in self.E}
        self.pending = {k: False for k in self.E}
        self.seen = {k: {} for k in self.E}
        self.lastw = {}
        self.readers = {}
        self.dsem = {}
        self.dcnt = {}
        self.semobj = dict(self.sem)
        self.cnt_misc = 0

    def _dma_sem(self, key):
        if key not in self.dsem:
            self.dsem[key] = self.es.enter_context(self.nc.semaphore("d_" + key))
            self.dcnt[key] = 0
            self.semobj["d_" + key] = self.dsem[key]
        return self.dsem[key]

    def emit(self, eng, fn, r=(), w=(), sig=True, dma=None):
        deps = {}

        def add(tok):
            if tok is None:
                return
            name, val, teng = tok
            if teng == eng and eng == "pe" and not name.startswith("d_"):
                return
            if deps.get(name, 0) < val:
                deps[name] = val

        for k in r:
            add(self.lastw.get(k))
        for k in w:
            add(self.lastw.get(k))
            for t in self.readers.get(k, ()):
                add(t)
        E = self.E[eng]
        for name, val in deps.items():
            if self.seen[eng].get(name, 0) < val:
                E.wait_ge(self.semobj[name], val)
                self.seen[eng][name] = val
        ins = fn(E)
        if dma is not None:
            s = self._dma_sem(dma)
            self.dcnt[dma] += 16
            ins.then_inc(s, 16)
            tok = ("d_" + dma, self.dcnt[dma], eng)
        elif sig:
            self.cnt[eng] += 1
            ins.then_inc(self.sem[eng], 1)
            self.pending[eng] = False
            tok = (eng, self.cnt[eng], eng)
        else:
            self.pending[eng] = True
            tok = (eng, self.cnt[eng] + 1, eng)
        for k in r:
            self.readers.setdefault(k, []).append(tok)
        for k in w:
            self.lastw[k] = tok
            self.readers[k] = []
        return ins

    def barrier(self):
        for e in self.E:
            assert not self.pending[e], e
        for e, E in self.E.items():
            for e2 in self.E:
                if e2 != e and self.seen[e].get(e2, 0) < self.cnt[e2]:
                    E.wait_ge(self.sem[e2], self.cnt[e2])
                    self.seen[e][e2] = self.cnt[e2]
            for key, s in self.dsem.items():
                name = "d_" + key
                if self.seen[e].get(name, 0) < self.dcnt[key]:
                    E.wait_ge(s, self.dcnt[key])
                    self.seen[e][name] = self.dcnt[key]


def fap(ap, *dims, parts=None):
    pp = list(ap.ap[0])
    if parts is not None:
        pp[1] = parts
    return AP(ap.tensor, ap.offset, [pp] + [list(d) for d in dims])


def build(nseq=NSEQ, pairs_a=4, pairs_b=4, phase2=True, ngrp2=TOK // 512):
    nc = bass.Bass("TRN2", target_bir_lowering=False)
    dt = nc.dram_tensor
    x = dt("x", [TOK, DM], F32, kind="ExternalInput").ap()
    attn_g = dt("attn_norm_g", [DM], F32, kind="ExternalInput").ap()
    w_in = dt("w_in", [DM, 1952], F32, kind="ExternalInput").ap()
    mqg = dt("moba_q_norm_g", [64], F32, kind="ExternalInput").ap()
    mkg = dt("moba_k_norm_g", [64], F32, kind="ExternalInput").ap()
    qag = dt("mla_q_a_norm_g", [256], F32, kind="ExternalInput").ap()
    w_qup = dt("w_q_up", [256, 768], F32, kind="ExternalInput").ap()
    kvag = dt("mla_kv_a_norm_g", [128], F32, kind="ExternalInput").ap()
    w_kvup = dt("w_kv_up", [128, 1024], F32, kind="ExternalInput").ap()
    lqg = dt("mla_q_norm_g", [96], F32, kind="ExternalInput").ap()
    lkg = dt("mla_k_norm_g", [96], F32, kind="ExternalInput").ap()
    w_o = dt("w_o", [DM, DM], F32, kind="ExternalInput").ap()
    ffn_g = dt("ffn_norm_g", [DM], F32, kind="ExternalInput").ap()
    w_gate = dt("w_gate", [DM, DFF], F32, kind="ExternalInput").ap()
    w_up = dt("w_up", [DM, DFF], F32, kind="ExternalInput").ap()
    w_down = dt("w_down", [DFF, DM], F32, kind="ExternalInput").ap()
    cosA_d = dt("cosA_tab", [SEQ, 32], F32, kind="ExternalInput").ap()
    sinA_d = dt("sinA_tab", [SEQ, 32], F32, kind="ExternalInput").ap()
    cosB_d = dt("cosB_tab", [SEQ, 16], F32, kind="ExternalInput").ap()
    sinB_d = dt("sinB_tab", [SEQ, 16], F32, kind="ExternalInput").ap()
    blk_d = dt("blkind", [8, SEQ], F32, kind="ExternalInput").ap()
    y = dt("y", [TOK, DM], F32, kind="ExternalOutput").ap()
    lscr = dt("lscr", [8, 512], F32, kind="Internal").ap()
    rscr = dt("rscr", [8, 512], F32, kind="Internal").ap()
    wg_bf = dt("wg_bf", [DM, DFF], BF16, kind="Internal").ap()
    wu_bf = dt("wu_bf", [DM, DFF], BF16, kind="Internal").ap()
    wd_bf = dt("wd_bf", [DFF, DM], BF16, kind="Internal").ap()

    with ExitStack() as es0:
        S = Sched(nc, es0)
        em = S.emit

        def sb(es, name, shape, dtype):
            return es.enter_context(nc.sbuf_tensor(name, shape, dtype))

        PSX = [es0.enter_context(nc.psum_tensor("psx%d" % i, [128, 2048], F32)) for i in range(2)]
        PS = [PSX[i // 2][:, (i % 2) * 1024:(i % 2 + 1) * 1024] for i in range(4)]
        PSb = [p.bitcast(BF16) for p in PS]
        PJ = PSX[0][:].rearrange("p (i n) -> p i n", n=512)
        assert tuple(PSb[1].shape) == (128, 2048), PSb[1].shape

        identf = sb(es0, "identf", [128, 128], F32)
        ident = sb(es0, "ident", [128, 128], BF16)
        gA = sb(es0, "gA", [128, 8], F32)
        gF = sb(es0, "gF", [128, 8], F32)
        epsc = sb(es0, "epsc", [128, 1], F32)
        em("pool", lambda e: e.memset(epsc[:], EPS), w=["epsc"])
        em("pool", lambda e: e.memset(identf[:], 1.0), w=["identf"])
        em("pool", lambda e: e.affine_select(out=identf[:], in_=identf[:], pattern=[[-1, 128]],
                                             compare_op=ALU.is_equal, fill=0.0, base=0, channel_multiplier=1),
           r=["identf"], w=["identf"])
        em("dve", lambda e: e.tensor_copy(out=ident[:], in_=identf[:]), r=["identf"], w=["ident"])

        def rms_rstd(dst, src, n, key):
            em("act", lambda e: e.activation(out=dst, in_=src, func=AF.Ln, scale=1.0 / n, bias=epsc[0:dst.shape[0], 0:1]),
               r=[key, "epsc"], w=[key])
            em("act", lambda e: e.activation(out=dst, in_=dst, func=AF.Exp, scale=-0.5), r=[key], w=[key])

        with ExitStack() as es1:
            wA = sb(es1, "wA", [128, 4, 8, 384], BF16)
            wB = sb(es1, "wB", [128, 8, 416], BF16)
            wq = sb(es1, "wq", [128, 2, 768], BF16)
            wkv = sb(es1, "wkv", [128, 1024], BF16)
            hnT = sb(es1, "hnT", [128, 8, SEQ], BF16)
            wo = hnT[:, 0:4, :].rearrange("p a (b n) -> p (a b) n", b=2)
            cqnT = sb(es1, "cqnT", [128, 2, SEQ], BF16)
            ckvnT = sb(es1, "ckvnT", [128, SEQ], BF16)
            kr = sb(es1, "kr", [128, NT, 32], F32)
            sskpe = sb(es1, "sskpe", [128, NT], F32)
            QT = sb(es1, "QT", [96, 2, SEQ], BF16)
            KT = sb(es1, "KT", [96, 2, SEQ], BF16)
            Vb = sb(es1, "Vb", [128, NT, 192], BF16)
            OT = sb(es1, "OT", [128, 8, SEQ], BF16)
            Pb = [sb(es1, "Pb%d" % i, [128, 1024], BF16) for i in range(3)]
            xbuf = [sb(es1, "xbuf%d" % i, [128, DM], F32) for i in range(2)]
            xs = [sb(es1, "xs%d" % i, [128, DM], BF16) for i in range(2)]
            sqA_full = sb(es1, "sqA", [128, 1280], F32)
            sqA = sqA_full[:, 0:1024]
            sqk = sqA_full[:, 768:1280].rearrange("p (i h d) -> p i h d", i=4, h=2)
            ssA = sb(es1, "ssA", [128, 8], F32)
            xn = xbuf
            rt = [sb(es1, "rt%d" % i, [128, 512], F32) for i in range(4)]
            qkb = [sb(es1, "qkb%d" % i, [128, 4, 4, 96], BF16) for i in range(2)]
            qtok = sb(es1, "qtok", [128, 8, 2, 72], BF16)
            tabA = sb(es1, "tabA", [128, NT, 2, 4, 32], F32)
            cosB = sb(es1, "cosB", [128, NT, 16], F32)
            sinB = sb(es1, "sinB", [128, NT, 16], F32)
            gqkA = sb(es1, "gqkA", [128, 2, 64], F32)
            gropeB = sb(es1, "gropeB", [128, 2, 32], F32)
            gvec = sb(es1, "gvec", [96, 2], F32)
            gqa = sb(es1, "gqa", [128, 2], F32)
            gkva = sb(es1, "gkva", [128, 1], F32)
            trim = sb(es1, "trim", [128, 128], BF16)
            gsb = sb(es1, "gsb", [128, 128], F32)
            trimf = gsb
            NB_ = 3
            Osb = [sb(es1, "Osb%d" % i, [128, 512], F32) for i in range(NB_)]
            lcol = [sb(es1, "lcol%d" % i, [128, 4], F32) for i in range(NB_)]
            Lsb = [sb(es1, "Lsb%d" % i, [128, 512], F32) for i in range(NB_)]
            print("phase1 sbuf remaining", nc.sbuf_bytes_remaining)
            kmf = sb(es1, "kmf", [64, 2, 8], F32)
            kmb = sb(es1, "kmb", [64, 2, 8], BF16)
            cmp_t = sqA_full[:, 0:256].rearrange("p (a b c) -> p a b c", a=4, b=8)
            cnt_t = sqA_full[:, 256:288].rearrange("p (a b) -> p a b", a=4)
            cqb = qkb[1][:, 0, :, :].rearrange("p a b -> p (a b)")
            kpg = sb(es1, "kpg", [128, 32], F32)
            ssq = sb(es1, "ssq", [128, 16], F32)

            for t in range(2):
                em("sp", lambda e, t=t: e.dma_start(out=xbuf[t][:], in_=x[t * 128:(t + 1) * 128, :]),
                   w=["xbuf%d" % t], dma="xbuf%d" % t)
            em("sp", lambda e: e.dma_start(out=gA[:], in_=attn_g.rearrange("(c p) -> p c", p=128),
                                           allow_slow_non_contiguous=True), w=["gA"], dma="gA")
            em("sp", lambda e: e.dma_start(out=gF[:], in_=ffn_g.rearrange("(c p) -> p c", p=128),
                                           allow_slow_non_contiguous=True), w=["gF"], dma="gF")
            for p in range(4):
                for j in range(3):
                    c0 = j * 512 + p * 128
                    em("pool", lambda e, p=p, j=j, c0=c0: e.dma_start(
                        out=wA[:, p, :, j * 128:(j + 1) * 128],
                        in_=w_in[:, c0:c0 + 128].rearrange("(c q) n -> q c n", q=128)),
                       w=["wA%d" % p], dma="wA%d" % p)
            em("pool", lambda e: e.dma_start(out=wB[:], in_=w_in[:, 1536:1952].rearrange("(c q) n -> q c n", q=128)),
               w=["wB"], dma="wB")
            em("pool", lambda e: e.dma_start(out=wq[:], in_=w_qup.rearrange("(c q) n -> q c n", q=128)),
               w=["wq"], dma="wq")
            em("pool", lambda e: e.dma_start(out=wkv[:], in_=w_kvup), w=["wkv"], dma="wkv")
            for (dst, src, key) in ((cosB, cosB_d, "cosB"), (sinB, sinB_d, "sinB")):
                em("sp", lambda e, dst=dst, src=src: e.dma_start(
                    out=dst[:], in_=src.rearrange("(t p) d -> p t d", p=128)), w=[key], dma=key)
            cosA = Osb[0][:].rearrange("p (t d) -> p t d", d=32)
            sinA = Osb[1][:].rearrange("p (t d) -> p t d", d=32)
            em("sp", lambda e: e.dma_start(out=cosA, in_=cosA_d.rearrange("(t p) d -> p t d", p=128)), w=["Osb0"], dma="cosA")
            em("sp", lambda e: e.dma_start(out=sinA, in_=sinA_d.rearrange("(t p) d -> p t d", p=128)), w=["Osb1"], dma="sinA")
            em("sp", lambda e: e.dma_start(out=gqkA[:, 0, :], in_=mqg.partition_broadcast(128)), w=["gqkA"], dma="gqkA")
            em("sp", lambda e: e.dma_start(out=gqkA[:, 1, :], in_=mkg.partition_broadcast(128)), w=["gqkA"], dma="gqkA")
            for a_ in range(2):
                for kind, (trig, tk, h0) in enumerate(((cosA, "Osb0", 0), (sinA, "Osb1", 32), (cosA, "Osb0", 32), (sinA, "Osb1", 0))):
                    em("dve", lambda e, a_=a_, kind=kind, trig=trig, h0=h0: e.tensor_tensor(
                        out=tabA[:, :, a_, kind, :], in0=trig, in1=fap(gqkA[:, a_, h0:h0 + 1], [0, NT], [1, 32]), op=ALU.mult),
                       r=[tk, "gqkA"], w=["tabA"])
            em("sp", lambda e: e.dma_start(out=gropeB[:, 0, :], in_=lqg[64:96].partition_broadcast(128)),
               w=["gropeB"], dma="gropeB")
            em("sp", lambda e: e.dma_start(out=gropeB[:, 1, :], in_=lkg[64:96].partition_broadcast(128)),
               w=["gropeB"], dma="gropeB")
            em("pool", lambda e: e.memset(gvec[:], 1.0), w=["gvec"])
            em("sp", lambda e: e.dma_start(out=gvec[0:64, 0:1], in_=lqg[0:64].rearrange("(p o) -> p o", o=1)),
               w=["gvec"], dma="gvec")
            em("sp", lambda e: e.dma_start(out=gvec[0:64, 1:2], in_=lkg[0:64].rearrange("(p o) -> p o", o=1)),
               w=["gvec"], dma="gvec")
            em("sp", lambda e: e.dma_start(out=gqa[:], in_=qag.rearrange("(c p) -> p c", p=128),
                                           allow_slow_non_contiguous=True), w=["gqa"], dma="gqa")
            em("sp", lambda e: e.dma_start(out=gkva[:], in_=kvag.rearrange("(p o) -> p o", o=1)),
               w=["gkva"], dma="gkva")
            em("pool", lambda e: e.memset(trimf[:], 0.0), w=["gsb"])
            em("pool", lambda e: e.affine_select(out=trimf[:], in_=trimf[:], pattern=[[1, 128]],
                                                 compare_op=ALU.is_ge, fill=NEG, base=0, channel_multiplier=-1),
               r=["gsb"], w=["gsb"])
            em("dve", lambda e: e.tensor_copy(out=trim[:], in_=trimf[:]), r=["gsb"], w=["trim"])
            em("pool", lambda e: e.memset(Vb[:, :, 64:128], 0.0), w=["Vb"])
            em("pool", lambda e: e.memset(Vb[:, :, 64:65], 1.0), w=["Vb"])

            if pairs_a < 4 or pairs_b < 4:
                em("pool", lambda e: e.memset(OT[:], 0.0), w=["OT"])
            cosBb = lambda t, n: fap(cosB[:, t, 0:1], [0, n], [1, 16])
            sinBb = lambda t, n: fap(sinB[:, t, 0:1], [0, n], [1, 16])

            def rope(src_x1, src_x2, cb, sb_, dst1, dst2, tmp, rkeys, wkeys, tabkeys):
                t1, t2 = tmp
                em("pool", lambda e: e.tensor_tensor(out=t1, in0=src_x1, in1=cb, op=ALU.mult), r=rkeys + tabkeys, w=["rt0"])
                em("pool", lambda e: e.tensor_tensor(out=t2, in0=src_x2, in1=sb_, op=ALU.mult), r=rkeys + tabkeys, w=["rt1"])
                em("pool", lambda e: e.tensor_tensor(out=dst1, in0=t1, in1=t2, op=ALU.subtract), r=["rt0", "rt1"], w=wkeys)
                em("pool", lambda e: e.tensor_tensor(out=t1, in0=src_x2, in1=cb, op=ALU.mult), r=rkeys + tabkeys, w=["rt0"])
                em("pool", lambda e: e.tensor_tensor(out=t2, in0=src_x1, in1=sb_, op=ALU.mult), r=rkeys + tabkeys, w=["rt1"])
                em("pool", lambda e: e.tensor_tensor(out=dst2, in0=t1, in1=t2, op=ALU.add), r=["rt0", "rt1"], w=wkeys)

            norm_pending = []

            def norm_tail(k, rows, chunk, g):
                osb, lsb, lc = Osb[k], Lsb[k], lcol[k]
                okey2, lkey2, ckey = "Osb%d" % k, "Lsb%d" % k, "lcol%d" % k
                em("dve", lambda e: e.reciprocal(out=lc[:], in_=lc[:]), r=[ckey], w=[ckey])
                em("sp", lambda e: e.dma_start(out=rscr[k, :].rearrange("(p f) -> p f", f=4), in_=lc[:]),
                   r=[ckey], w=["rscr%d" % k], dma="ls%d" % k)
                em("sp", lambda e: e.dma_start(out=lsb[rows, :], in_=rscr[k, :].partition_broadcast(64)),
                   r=["rscr%d" % k], w=[lkey2], dma="ls%d" % k)
                em("pool", lambda e: e.tensor_tensor(out=OT[rows, chunk, g * 512:(g + 1) * 512],
                                                     in0=osb[rows, :], in1=lsb[rows, :], op=ALU.mult),
                   r=[okey2, lkey2], w=["OT"])

            def norm_flush():
                while norm_pending:
                    norm_tail(*norm_pending.pop(0))

            def attention(R, scale, chunk):
                skeys = lambda k: ["PL", "PM"] if k == "PL" else [k]
                jobs = []
                for hh in range(2):
                    for g in range(4):
                        nk = 4 * g + 4
                        for j in range(nk // 2):
                            jobs.append((hh, g, j, j == nk // 2 - 1))

                def s_stage(idx):
                    hh, g, j, last = jobs[idx]
                    Sb = [PS[0], PS[1], PS[3]][idx % 3]
                    skey = ["PS0", "PS1", "PL"][idx % 3]
                    for u in range(2):
                        kt = 2 * j + u
                        c0 = max(0, kt - 4 * g) * 128
                        diag = kt >= 4 * g
                        em("pe", lambda e, u=u, kt=kt, c0=c0, diag=diag: e.matmul(
                            Sb[:, u * 512 + c0:(u + 1) * 512],
                            lhsT=KT[0:R, hh, kt * 128:(kt + 1) * 128],
                            rhs=QT[0:R, hh, g * 512 + c0:(g + 1) * 512],
                            start=True, stop=not diag),
                           r=["QT", "KT"], w=skeys(skey), sig=(u == 1 and not diag))
                        if diag:
                            em("pe", lambda e, u=u, c0=c0: e.matmul(
                                Sb[:, u * 512 + c0:u * 512 + c0 + 128], lhsT=ident[:], rhs=trim[:],
                                start=False, stop=True),
                               r=["ident", "trim"], w=skeys(skey), sig=(u == 1))
                    kt0 = 2 * j
                    cmin = max(0, kt0 - 4 * g) * 128
                    Pt = Pb[idx % 3]
                    pkey = "Pb%d" % (idx % 3)
                    em("act", lambda e: e.activation(
                        out=fap(Pt[:, cmin:cmin + 1], [512, 2], [1, 512 - cmin]),
                        in_=fap(Sb[:, cmin:cmin + 1], [512, 2], [1, 512 - cmin]),
                        func=AF.Exp, scale=scale), r=skeys(skey), w=[pkey])

                def pv_stage(idx):
                    hh, g, j, last = jobs[idx]
                    rows = slice(0, 64) if hh == 0 else slice(64, 128)
                    lp = 64 if hh == 0 else 0
                    vsl = slice(0, 65) if hh == 0 else slice(64, 192)
                    orows = 65 if hh == 0 else 128
                    Pt = Pb[idx % 3]
                    pkey = "Pb%d" % (idx % 3)
                    ob = g % 2
                    Ob = PS[2][:, ob * 512:(ob + 1) * 512]
                    okey = "PO%d" % ob
                    for u in range(2):
                        kt = 2 * j + u
                        c0 = max(0, kt - 4 * g) * 128
                        fin = last and u == 1
                        em("pe", lambda e, u=u, kt=kt, c0=c0: e.matmul(
                            Ob[0:orows, c0:512], lhsT=Vb[:, kt, vsl], rhs=Pt[:, u * 512 + c0:(u + 1) * 512],
                            start=(kt == 0), stop=(kt == 4 * g + 3)),
                           r=[pkey, "Vb"], w=[okey], sig=fin)
                    if last:
                        k = S.cnt_misc % NB_
                        S.cnt_misc += 1
                        osb, lc = Osb[k], lcol[k]
                        okey2, ckey = "Osb%d" % k, "lcol%d" % k
                        em("dve", lambda e: e.tensor_copy(out=osb[0:orows, :], in_=Ob[0:orows, :]), r=[okey], w=[okey2])
                        em("sp", lambda e: e.dma_start(out=lscr[k:k + 1, :], in_=osb[lp:lp + 1, :]),
                           r=[okey2], w=["lscr%d" % k], dma="ls%d" % k)
                        em("sp", lambda e: e.dma_start(out=lc[:], in_=lscr[k, :].rearrange("(p f) -> p f", f=4)),
                           r=["lscr%d" % k], w=[ckey], dma="ls%d" % k)
                        norm_pending.append((k, rows, chunk, g))
                        while len(norm_pending) > 1:
                            norm_tail(*norm_pending.pop(0))

                n = len(jobs)
                s_stage(0)
                s_stage(1)
                for idx in range(n):
                    if idx + 2 < n:
                        s_stage(idx + 2)
                    pv_stage(idx)

            conv_jobs = []
            for q4 in range(4):
                conv_jobs.append((wg_bf[q4 * 256:(q4 + 1) * 256, :], w_gate[q4 * 256:(q4 + 1) * 256, :], "wgbf"))
                conv_jobs.append((wu_bf[q4 * 256:(q4 + 1) * 256, :], w_up[q4 * 256:(q4 + 1) * 256, :], "wubf"))
            for q4 in range(4):
                conv_jobs.append((wd_bf[q4 * 704:(q4 + 1) * 704, :], w_down[q4 * 704:(q4 + 1) * 704, :], "wdbf"))

            def conv_step():
                if conv_jobs:
                    dst, src, key = conv_jobs.pop(0)
                    em("pool", lambda e: e.dma_start(out=dst, in_=src), w=[key], dma=key)

            for s in range(nseq):
                tok0 = s * SEQ
                def A1(t):
                    xb = xbuf[t % 2]
                    xk = "xbuf%d" % (t % 2)
                    col = ssA[:, 3 + t % 4:4 + t % 4]
                    if not (s == 0 and t < 2):
                        em("sp", lambda e: e.dma_start(out=xb[:], in_=x[tok0 + t * 128: tok0 + (t + 1) * 128, :]),
                           w=[xk], dma=xk)
                    em("act", lambda e: e.activation(out=xs[t % 2][:], in_=xb[:], func=AF.Square, accum_out=col),
                       r=[xk], w=["xs%d" % (t % 2), "ssA%d" % (t % 4)])
                    rms_rstd(col, col, DM, "ssA%d" % (t % 4))

                def A2(t):
                    xb = xbuf[t % 2]
                    xk = "xbuf%d" % (t % 2)
                    col = ssA[:, 3 + t % 4:4 + t % 4]
                    xsb = xs[t % 2]
                    xsk = "xs%d" % (t % 2)
                    em("dve", lambda e: e.tensor_scalar(out=xsb[:], in0=xb[:], scalar1=col, scalar2=None, op0=ALU.mult),
                       r=[xk, "ssA%d" % (t % 4)], w=[xsk])
                    tt = t % 4
                    for c in range(8):
                        em("pe", lambda e, c=c: e.transpose(
                            out=PSb[c // 4][:, (c % 4) * 512 + tt * 128:(c % 4) * 512 + (tt + 1) * 128],
                            in_=xsb[:, c * 128:(c + 1) * 128], identity=ident[:]),
                           r=[xsk, "ident"], w=["PS%d" % (c // 4)], sig=(c == 7))
                    if tt == 3:
                        g0 = (t // 4) * 512
                        for c in range(8):
                            em("dve", lambda e, c=c: e.tensor_scalar(
                                out=hnT[:, c, g0:g0 + 512], in0=PSb[c // 4][:, (c % 4) * 512:(c % 4 + 1) * 512],
                                scalar1=gA[:, c:c + 1], scalar2=None, op0=ALU.mult),
                               r=["PS%d" % (c // 4), "gA"], w=["hnT"])

                for i in range(NT + 1):
                    if i < NT:
                        A1(i)
                    if i >= 1:
                        A2(i - 1)

                def B_pe(t):
                        pb = t % 2
                        Pp = PS[2][:, pb * 512:pb * 512 + 416]
                        pk = "PO%d" % pb
                        for c in range(8):
                            em("pe", lambda e, c=c, t=t, Pp=Pp: e.matmul(
                                Pp, lhsT=hnT[:, c, t * 128:(t + 1) * 128], rhs=wB[:, c, :], start=(c == 0), stop=(c == 7)),
                               r=["hnT", "wB"], w=[pk], sig=(c == 7))

                def B_mid(t):
                        pb = t % 2
                        Pp = PS[2][:, pb * 512:pb * 512 + 416]
                        pk = "PO%d" % pb
                        em("act", lambda e, Pp=Pp: e.activation(out=sqA[:, 0:416], in_=Pp, func=AF.Square), r=[pk], w=["sqA"])
                        em("dve", lambda e: e.reduce_sum(out=ssA[:, 1:2], in_=sqA[:, 0:256], axis=AX.X), r=["sqA"], w=["ssB"])
                        em("dve", lambda e: e.reduce_sum(out=ssA[:, 2:3], in_=sqA[:, 256:384], axis=AX.X), r=["sqA"], w=["ssB"])
                        em("dve", lambda e, t=t: e.reduce_sum(out=sskpe[:, t:t + 1], in_=sqA[:, 384:416], axis=AX.X),
                           r=["sqA"], w=["sskpe"])
                        rms_rstd(ssA[:, 1:2], ssA[:, 1:2], 256, "ssB")
                        rms_rstd(ssA[:, 2:3], ssA[:, 2:3], 128, "ssB")
                        em("dve", lambda e, Pp=Pp: e.tensor_scalar(out=cqb[:, 0:256], in0=Pp[:, 0:256], scalar1=ssA[:, 1:2],
                                                                  scalar2=None, op0=ALU.mult), r=[pk, "ssB"], w=["qkb1"])
                        em("dve", lambda e, Pp=Pp: e.tensor_scalar(out=cqb[:, 256:384], in0=Pp[:, 256:384], scalar1=ssA[:, 2:3],
                                                                  scalar2=None, op0=ALU.mult), r=[pk, "ssB"], w=["qkb1"])
                        em("dve", lambda e, Pp=Pp: e.tensor_tensor(out=kpg[:], in0=Pp[:, 384:416], in1=gropeB[:, 1, :], op=ALU.mult),
                           r=[pk, "gropeB"], w=["kpg"])
                        rope(kpg[:, 0:16], kpg[:, 16:32], cosB[:, t, :], sinB[:, t, :],
                             kr[:, t, 0:16], kr[:, t, 16:32],
                             [rt[i][:, 0:16] for i in range(2)], ["kpg"], ["kr"], ["cosB", "sinB"])

                def B_tr(t):
                        PT = PSb[3][:, 1024:1024 + 384]
                        for c in range(3):
                            em("pe", lambda e, c=c, PT=PT: e.transpose(out=PT[:, c * 128:(c + 1) * 128],
                                                                      in_=cqb[:, c * 128:(c + 1) * 128], identity=ident[:]),
                               r=["qkb1", "ident"], w=["PM"], sig=(c == 2))
                        for c in range(2):
                            em("dve", lambda e, c=c, t=t, PT=PT: e.tensor_scalar(
                                out=cqnT[:, c, t * 128:(t + 1) * 128], in0=PT[:, c * 128:(c + 1) * 128],
                                scalar1=gqa[:, c:c + 1], scalar2=None, op0=ALU.mult), r=["PM", "gqa"], w=["cqnT"])
                        em("dve", lambda e, t=t, PT=PT: e.tensor_scalar(
                            out=ckvnT[:, t * 128:(t + 1) * 128], in0=PT[:, 256:384],
                            scalar1=gkva[:, 0:1], scalar2=None, op0=ALU.mult), r=["PM", "gkva"], w=["ckvnT"])

                B_pe(0)
                for t in range(NT):
                    if t + 1 < NT:
                        B_pe(t + 1)
                    B_mid(t)
                    B_tr(t)

                em("pool", lambda e: e.dma_start(out=KT[64:72, 0, :], in_=blk_d), w=["KT"], dma="blk")
                em("pool", lambda e: e.dma_start(out=KT[64:72, 1, :], in_=blk_d), w=["KT"], dma="blk")
                em("pool", lambda e: e.memset(QT[64:72, :, :], 0.0), w=["QT"])
                for p in range(pairs_a):
                    def moba_s1(b, p=p):
                        for i in range(4):
                            t = 4 * b + i
                            for c in range(8):
                                em("pe", lambda e, c=c, t=t, i=i: e.matmul(
                                    PJ[:, i, 0:384], lhsT=hnT[:, c, t * 128:(t + 1) * 128], rhs=wA[:, p, c, :],
                                    start=(c == 0), stop=(c == 7)),
                                   r=["hnT", "wA%d" % p], w=["PS%d" % (i // 2)], sig=(c == 7))

                    def moba_s1b(b, p=p):
                        pk = ["PS0", "PS1"]
                        xnb = xn[b % 2]
                        xnk = "xbuf%d" % (b % 2)
                        em("act", lambda e: e.activation(out=sqA[:].rearrange("p (i n) -> p i n", n=256),
                                                         in_=PJ[:, :, 0:256], func=AF.Square), r=pk, w=["sqA"])
                        em("dve", lambda e: e.reduce_sum(out=ssq[:], in_=sqA[:].rearrange("p (h d) -> p h d", d=64),
                                                         axis=AX.X), r=["sqA"], w=["ssq"])
                        rms_rstd(ssq[:], ssq[:], 64, "ssq")
                        em("dve", lambda e: e.tensor_tensor(
                            out=fap(xnb[:, 0:1], [256, 4], [64, 4], [1, 64]),
                            in0=fap(PJ[:, 0, 0:1], [512, 4], [64, 4], [1, 64]),
                            in1=fap(ssq[:, 0:1], [4, 4], [1, 4], [0, 64]), op=ALU.mult), r=pk + ["ssq"], w=[xnk])
                        em("act", lambda e: e.activation(
                            out=fap(Vb[:, 4 * b, 0:1], [192, 4], [128, 2], [1, 64]),
                            in_=fap(PJ[:, 0, 256:257], [512, 4], [64, 2], [1, 64]), func=AF.Copy),
                           r=pk, w=["Vb"])

                    def moba_s2(b, p=p):
                        xnb = xn[b % 2]
                        xnk = "xbuf%d" % (b % 2)
                        qb = qkb[b % 2]
                        qk_ = "qkb%d" % (b % 2)
                        x1 = fap(xnb[:, 0:1], [256, 4], [128, 2], [64, 2], [1, 32])
                        x2 = fap(xnb[:, 32:33], [256, 4], [128, 2], [64, 2], [1, 32])
                        tab = lambda kind: fap(tabA[:, 4 * b, 0, kind, 0:1], [256, 4], [128, 2], [0, 2], [1, 32])
                        tv = [rt[i][:].rearrange("p (i a h d) -> p i a h d", i=4, a=2, h=2) for i in range(4)]
                        d1 = fap(qb[:, 0, 0, 0:1], [384, 4], [192, 2], [96, 2], [1, 32])
                        d2 = fap(qb[:, 0, 0, 32:33], [384, 4], [192, 2], [96, 2], [1, 32])
                        em("pool", lambda e: e.tensor_tensor(out=tv[0], in0=x1, in1=tab(0), op=ALU.mult), r=[xnk, "tabA"], w=["rt0"])
                        em("pool", lambda e: e.tensor_tensor(out=tv[1], in0=x2, in1=tab(1), op=ALU.mult), r=[xnk, "tabA"], w=["rt1"])
                        em("pool", lambda e: e.tensor_tensor(out=d1, in0=tv[0], in1=tv[1], op=ALU.subtract), r=["rt0", "rt1"], w=[qk_])
                        em("dve", lambda e: e.tensor_tensor(out=tv[2], in0=x2, in1=tab(2), op=ALU.mult), r=[xnk, "tabA"], w=["rt2"])
                        em("dve", lambda e: e.tensor_tensor(out=tv[3], in0=x1, in1=tab(3), op=ALU.mult), r=[xnk, "tabA"], w=["rt3"])
                        em("dve", lambda e: e.tensor_tensor(out=d2, in0=tv[2], in1=tv[3], op=ALU.add), r=["rt2", "rt3"], w=[qk_])

                    def moba_kmean(b):
                        em("dve", lambda e: e.reduce_sum(
                            out=kmf[:, :, 2 * b:2 * b + 2],
                            in_=KT[0:64, :, 4 * b * 128:(4 * b + 4) * 128].rearrange("p h (b k) -> p h b k", k=256),
                            axis=AX.X), r=["KT"], w=["kmf"])

                    def moba_s2b(b, p=p):
                        qb = qkb[b % 2]
                        qk_ = "qkb%d" % (b % 2)
                        if b >= 2:
                            em("act", lambda e: e.activation(out=qtok[:, 4 * (b - 2):4 * (b - 2) + 4, :, 0:64],
                                                             in_=qb[:, :, 0:2, 0:64], func=AF.Copy), r=[qk_], w=["qtok"])
                        PT = PSb[3].rearrange("p (h k) -> p h k", k=128)
                        for i in range(4):
                            for j in range(4):
                                em("pe", lambda e, i=i, j=j: e.transpose(out=PT[0:64, i * 4 + j, :], in_=qb[:, i, j, 0:64],
                                                                         identity=ident[:]),
                                   r=[qk_, "ident"], w=["PL", "PM"], sig=(i == 3 and j == 3))
                        em("act", lambda e: e.activation(
                            out=fap(QT[0:64, 0, 4 * b * 128:4 * b * 128 + 1], [SEQ, 2], [128, 4], [1, 128]),
                            in_=fap(PT[0:64, 0, 0:1], [128, 2], [512, 4], [1, 128]), func=AF.Copy), r=["PL", "PM"], w=["QT"])
                        em("act", lambda e: e.activation(
                            out=fap(KT[0:64, 0, 4 * b * 128:4 * b * 128 + 1], [SEQ, 2], [128, 4], [1, 128]),
                            in_=fap(PT[0:64, 2, 0:1], [128, 2], [512, 4], [1, 128]), func=AF.Copy), r=["PL", "PM"], w=["KT"])

                    for b in range(5):
                        if b < 4:
                            moba_s1(b)
                        if b >= 1:
                            moba_s2(b - 1)
                        if b >= 2:
                            moba_kmean(b - 2)
                        if b < 4:
                            moba_s1b(b)
                        if b >= 1:
                            moba_s2b(b - 1)
                    moba_kmean(3)
                    em("dve", lambda e: e.tensor_scalar(out=kmb[:], in0=kmf[:], scalar1=1.0 / 256, scalar2=None, op0=ALU.mult),
                       r=["kmf"], w=["kmb"])
                    Gp = PS[3][:, 0:128]
                    for tl in range(8):
                        for hh in range(2):
                            em("pe", lambda e, tl=tl, hh=hh: e.matmul(
                                Gp[:, (tl * 2 + hh) * 8:(tl * 2 + hh) * 8 + 8],
                                lhsT=QT[0:64, hh, (8 + tl) * 128:(9 + tl) * 128], rhs=kmb[:, hh, :], start=True, stop=True),
                               r=["QT", "kmb"], w=["PL"], sig=(tl == 7 and hh == 1))
                    em("dve", lambda e: e.tensor_copy(out=gsb[:], in_=Gp), r=["PL"], w=["gsb"])
                    em("pool", lambda e: e.memset(qtok[:, :, :, 64:72], 0.0), w=["qtok"])
                    for b in range(4, 8):
                        tl = 2 * b - 8
                        base = gsb[:, tl * 16:tl * 16 + 1]
                        em("dve", lambda e, b=b, base=base: e.tensor_tensor(
                            out=cmp_t[:, :, 0:b, 0:b], in0=fap(base, [8, 4], [0, b], [1, b]),
                            in1=fap(base, [8, 4], [1, b], [0, b]), op=ALU.is_gt), r=["gsb"], w=["sqA"])
                        em("dve", lambda e, b=b: e.reduce_sum(out=cnt_t[:, :, 0:b], in_=cmp_t[:, :, 0:b, 0:b], axis=AX.X),
                           r=["sqA"], w=["sqA"])
                        em("dve", lambda e, b=b, tl=tl: e.tensor_scalar(
                            out=fap(qtok[:, tl, 0, 64:65], [72, 4], [1, b]), in0=cnt_t[:, :, 0:b],
                            scalar1=2.5, scalar2=NEG, op0=ALU.is_ge, op1=ALU.mult), r=["sqA"], w=["qtok"])
                    for half in range(2):
                        PT2 = PSb[half][:, 0:1024].rearrange("p (i k) -> p i k", k=128)
                        for i in range(8):
                            tl, hh = half * 4 + i // 2, i % 2
                            em("pe", lambda e, i=i, tl=tl, hh=hh, PT2=PT2: e.transpose(
                                out=PT2[0:72, i, :], in_=qtok[:, tl, hh, :], identity=ident[:]),
                               r=["qtok", "ident"], w=["PS%d" % half], sig=(i == 7))
                        for hh in range(2):
                            em("dve", lambda e, hh=hh, half=half, PT2=PT2: e.tensor_copy(
                                out=QT[64:72, hh, 1024 + half * 512:1024 + (half + 1) * 512].rearrange("p (t k) -> p t k", k=128),
                                in_=fap(PT2[64:72, hh, 0:1], [256, 4], [1, 128])), r=["PS%d" % half], w=["QT"])
                    conv_step()
                    attention(72, 0.125, p)

                sc_mla = 96.0 ** -0.5
                em("pool", lambda e: e.dma_start(out=wo, in_=w_o.rearrange("(c q) n -> q c n", q=128)),
                   w=["hnT"], dma="wo")
                for p in range(pairs_b):
                    def mla_s1(b, p=p):
                        for i in range(4):
                            t = 4 * b + i
                            for c in range(2):
                                em("pe", lambda e, c=c, t=t, i=i: e.matmul(
                                    PJ[:, i, 0:192], lhsT=cqnT[:, c, t * 128:(t + 1) * 128], rhs=wq[:, c, p * 192:(p + 1) * 192],
                                    start=(c == 0), stop=(c == 1)), r=["cqnT", "wq"], w=["PS%d" % (i // 2)], sig=False)
                            em("pe", lambda e, t=t, i=i: e.matmul(
                                PJ[:, i, 256:512], lhsT=ckvnT[:, t * 128:(t + 1) * 128], rhs=wkv[:, p * 256:(p + 1) * 256],
                                start=True, stop=True), r=["ckvnT", "wkv"], w=["PS%d" % (i // 2)], sig=True)

                    def mla_s1b(b, p=p):
                        pk = ["PS0", "PS1"]
                        xnb = xn[b % 2]
                        xnk = "xbuf%d" % (b % 2)
                        qb = qkb[b % 2]
                        qk_ = "qkb%d" % (b % 2)
                        knope = fap(PJ[:, 0, 256:257], [512, 4], [128, 2], [1, 64])
                        em("act", lambda e: e.activation(out=sqA[:, 0:768].rearrange("p (i n) -> p i n", n=192),
                                                         in_=PJ[:, :, 0:192], func=AF.Square), r=pk, w=["sqA"])
                        em("act", lambda e: e.activation(out=sqk, in_=knope, func=AF.Square), r=pk, w=["sqA"])
                        em("dve", lambda e: e.reduce_sum(out=ssq[:, 0:8], in_=sqA[:, 0:768].rearrange("p (h d) -> p h d", d=96),
                                                         axis=AX.X), r=["sqA"], w=["ssq"])
                        em("dve", lambda e: e.reduce_sum(out=ssq[:, 8:16], in_=sqk.rearrange("p i h d -> p (i h) d"),
                                                         axis=AX.X), r=["sqA"], w=["ssq"])
                        em("dve", lambda e: e.tensor_tensor(
                            out=ssq[:, 8:16].rearrange("p (i h) -> p i h", h=2), in0=ssq[:, 8:16].rearrange("p (i h) -> p i h", h=2),
                            in1=fap(sskpe[:, 4 * b:4 * b + 1], [1, 4], [0, 2]), op=ALU.add), r=["ssq", "sskpe"], w=["ssq"])
                        rms_rstd(ssq[:], ssq[:], 96, "ssq")
                        em("dve", lambda e: e.tensor_tensor(
                            out=fap(xnb[:, 0:1], [192, 4], [96, 2], [1, 96]),
                            in0=fap(PJ[:, 0, 0:1], [512, 4], [96, 2], [1, 96]),
                            in1=fap(ssq[:, 0:1], [2, 4], [1, 2], [0, 96]), op=ALU.mult), r=pk + ["ssq"], w=[xnk])
                        em("dve", lambda e: e.tensor_tensor(
                            out=qb[:, :, 2:4, 0:64], in0=knope, in1=fap(ssq[:, 8:9], [2, 4], [1, 2], [0, 64]), op=ALU.mult),
                           r=pk + ["ssq"], w=[qk_])
                        em("dve", lambda e: e.tensor_tensor(
                            out=qb[:, :, 2:4, 64:96], in0=fap(kr[:, 4 * b, 0:1], [32, 4], [0, 2], [1, 32]),
                            in1=fap(ssq[:, 8:9], [2, 4], [1, 2], [0, 32]), op=ALU.mult), r=["kr", "ssq"], w=[qk_])
                        em("act", lambda e: e.activation(
                            out=fap(Vb[:, 4 * b, 0:1], [192, 4], [128, 2], [1, 64]),
                            in_=fap(PJ[:, 0, 320:321], [512, 4], [128, 2], [1, 64]), func=AF.Copy), r=pk, w=["Vb"])

                    def mla_s2(b, p=p):
                        xnb = xn[b % 2]
                        xnk = "xbuf%d" % (b % 2)
                        qb = qkb[b % 2]
                        qk_ = "qkb%d" % (b % 2)
                        em("act", lambda e: e.activation(out=qb[:, :, 0:2, 0:64],
                                                         in_=fap(xnb[:, 0:1], [192, 4], [96, 2], [1, 64]), func=AF.Copy),
                           r=[xnk], w=[qk_])
                        em("pool", lambda e: e.tensor_tensor(
                            out=fap(xnb[:, 64:65], [192, 4], [96, 2], [1, 32]), in0=fap(xnb[:, 64:65], [192, 4], [96, 2], [1, 32]),
                            in1=fap(gropeB[:, 0, 0:1], [0, 4], [0, 2], [1, 32]), op=ALU.mult), r=[xnk, "gropeB"], w=[xnk])
                        x1 = fap(xnb[:, 64:65], [192, 4], [96, 2], [1, 16])
                        x2 = fap(xnb[:, 80:81], [192, 4], [96, 2], [1, 16])
                        cb = fap(cosB[:, 4 * b, 0:1], [16, 4], [0, 2], [1, 16])
                        sb_ = fap(sinB[:, 4 * b, 0:1], [16, 4], [0, 2], [1, 16])
                        tmp = [rt[i][:, 0:128].rearrange("p (i h d) -> p i h d", i=4, h=2) for i in range(2)]
                        rope(x1, x2, cb, sb_, qb[:, :, 0:2, 64:80], qb[:, :, 0:2, 80:96], tmp, [xnk], [qk_], ["cosB", "sinB"])

                    def mla_s2b(b, p=p):
                        qb = qkb[b % 2]
                        qk_ = "qkb%d" % (b % 2)
                        PT = PSb[3].rearrange("p (h k) -> p h k", k=128)
                        for i in range(4):
                            for j in range(4):
                                em("pe", lambda e, i=i, j=j: e.transpose(out=PT[0:96, i * 4 + j, :], in_=qb[:, i, j, :],
                                                                         identity=ident[:]),
                                   r=[qk_, "ident"], w=["PL", "PM"], sig=(i == 3 and j == 3))
                        em("act", lambda e: e.activation(
                            out=fap(QT[0:96, 0, 4 * b * 128:4 * b * 128 + 1], [SEQ, 2], [128, 4], [1, 128]),
                            in_=fap(PT[0:96, 0, 0:1], [128, 2], [512, 4], [1, 128]),
                            func=AF.Copy, scale=gvec[:, 0:1]), r=["PL", "PM", "gvec"], w=["QT"])
                        em("act", lambda e: e.activation(
                            out=fap(KT[0:96, 0, 4 * b * 128:4 * b * 128 + 1], [SEQ, 2], [128, 4], [1, 128]),
                            in_=fap(PT[0:96, 2, 0:1], [128, 2], [512, 4], [1, 128]),
                            func=AF.Copy, scale=gvec[:, 1:2]), r=["PL", "PM", "gvec"], w=["KT"])

                    for b in range(5):
                        if b < 4:
                            mla_s1(b)
                        if b >= 1:
                            mla_s2(b - 1)
                        if b < 4:
                            mla_s1b(b)
                        if b >= 1:
                            mla_s2b(b - 1)
                    conv_step()
                    attention(96, sc_mla, 4 + p)

                norm_flush()
                def D_load(t):
                    em("sp", lambda e: e.dma_start(out=xbuf[t % 2][:], in_=x[tok0 + t * 128: tok0 + (t + 1) * 128, :]),
                       w=["xbuf%d" % (t % 2)], dma="xbuf%d" % (t % 2))

                D_load(0)
                for t in range(NT):
                    xb = xbuf[t % 2]
                    xk = "xbuf%d" % (t % 2)
                    Po = PS[t % 2]
                    pk = "PS%d" % (t % 2)
                    if t >= 1 and t + 1 < NT:
                        D_load(t + 1)
                    for hf in range(2):
                        for c in range(8):
                            em("pe", lambda e, c=c, hf=hf, t=t, Po=Po: e.matmul(
                                Po[:, hf * 512:(hf + 1) * 512], lhsT=OT[:, c, t * 128:(t + 1) * 128],
                                rhs=wo[:, c, hf * 512:(hf + 1) * 512], start=(c == 0), stop=(c == 7)),
                               r=["OT", "hnT"], w=[pk], sig=(c == 7 and hf == 1))
                    em("dve", lambda e, xb=xb, Po=Po: e.tensor_tensor(out=xb[:], in0=Po[:], in1=xb[:], op=ALU.add),
                       r=[pk, xk], w=[xk])
                    em("sp", lambda e, xb=xb, t=t: e.dma_start(out=y[tok0 + t * 128: tok0 + (t + 1) * 128, :], in_=xb[:]),
                       r=[xk], w=["y%d" % (s * NT + t)], dma="st" + xk)
                    if t == 0 and NT > 1:
                        D_load(1)

        while conv_jobs:
            conv_step()
        S.barrier()
        with ExitStack() as es2:
            wg = sb(es2, "wg", [128, 8, DFF], BF16)
            wu = sb(es2, "wu", [128, 8, DFF], BF16)
            wd = sb(es2, "wd", [128, NFF, DM], BF16)
            hbn = [sb(es2, "hbn%d" % i, [128, DM], F32) for i in range(4)]
            hbd = [sb(es2, "hbd%d" % i, [128, DM], F32) for i in range(2)]
            xs2 = [sb(es2, "xsb%d" % i, [128, DM], BF16) for i in range(4)]
            ss2 = sb(es2, "ss2", [128, 8], F32)
            gnT = [sb(es2, "gnT%d" % i, [128, 8, 512], BF16) for i in range(2)]
            hmT = sb(es2, "hmT", [128, NFF, 512], BF16)
            sg = [sb(es2, "sg%d" % i, [128, 512], F32) for i in range(2)]
            print("phase2 sbuf remaining", nc.sbuf_bytes_remaining)
            NBLK = 4
            BW = DFF // NBLK
            wkeys = lambda f: ["wg%d" % j for j in range(NBLK) if f * 128 < (j + 1) * BW and (f + 1) * 128 > j * BW]
            ukeys = lambda f: ["wu%d" % j for j in range(NBLK) if f * 128 < (j + 1) * BW and (f + 1) * 128 > j * BW]
            ngr = ngrp2 if phase2 else 0
            st2 = {"n": 0}

            def norm_pre(grp):
                for tt in range(4):
                    gt = grp * 4 + tt
                    i = st2["n"] % 4
                    st2["n"] += 1
                    hbt, hk = hbn[i], "hbn%d" % i
                    col = ss2[:, (grp % 2) * 4 + tt:(grp % 2) * 4 + tt + 1]
                    ck = "ss2_%d" % ((grp % 2) * 4 + tt)
                    xsb = xs2[tt]
                    xsk = "xsb%d" % tt
                    em("sp", lambda e, hbt=hbt, gt=gt: e.dma_start(out=hbt[:], in_=y[gt * 128:(gt + 1) * 128, :]),
                       r=["y%d" % gt], w=[hk], dma=hk)
                    em("act", lambda e, hbt=hbt, xsb=xsb, col=col: e.activation(out=xsb[:], in_=hbt[:], func=AF.Square,
                                                                               accum_out=col), r=[hk], w=[xsk, ck])
                    rms_rstd(col, col, DM, ck)
                    em("dve", lambda e, hbt=hbt, xsb=xsb, col=col: e.tensor_scalar(out=xsb[:], in0=hbt[:], scalar1=col, scalar2=None,
                                                                                op0=ALU.mult), r=[hk, ck], w=[xsk])

            def norm_stage(grp):
                gb = gnT[grp % 2]
                gk = "gnT%d" % (grp % 2)
                for tt in range(4):
                    xsb = xs2[tt]
                    xsk = "xsb%d" % tt
                    for c in range(8):
                        em("pe", lambda e, c=c, tt=tt, xsb=xsb: e.transpose(
                            out=PSb[c // 4][:, (c % 4) * 512 + tt * 128:(c % 4) * 512 + (tt + 1) * 128],
                            in_=xsb[:, c * 128:(c + 1) * 128], identity=ident[:]),
                           r=[xsk, "ident"], w=["PS%d" % (c // 4)], sig=(c == 7))
                for c in range(8):
                    em("dve", lambda e, c=c: e.tensor_scalar(
                        out=gb[:, c, :], in0=PSb[c // 4][:, (c % 4) * 512:(c % 4 + 1) * 512],
                        scalar1=gF[:, c:c + 1], scalar2=None, op0=ALU.mult),
                       r=["PS%d" % (c // 4), "gF"], w=[gk])

            def gateup_stage(grp):
                gb = gnT[grp % 2]
                gk = "gnT%d" % (grp % 2)
                for f in range(NFF):
                    if grp == 0 and f == 2:
                        load_ffn_weights([2], False)
                    if grp == 0 and f == 7:
                        load_ffn_weights([3], False)
                    if grp == 0 and f == 12:
                        load_ffn_weights([], True)
                    Pg = PS[f % 2][:, 0:512]
                    Pu = PS[f % 2][:, 512:1024]
                    pk = "PS%d" % (f % 2)
                    for c in range(8):
                        em("pe", lambda e, c=c, f=f, Pg=Pg: e.matmul(Pg, lhsT=wg[:, c, f * 128:(f + 1) * 128], rhs=gb[:, c, :],
                                                                    start=(c == 0), stop=(c == 7)),
                           r=wkeys(f) + [gk], w=[pk], sig=False)
                    for c in range(8):
                        em("pe", lambda e, c=c, f=f, Pu=Pu: e.matmul(Pu, lhsT=wu[:, c, f * 128:(f + 1) * 128], rhs=gb[:, c, :],
                                                                    start=(c == 0), stop=(c == 7)),
                           r=ukeys(f) + [gk], w=[pk], sig=(c == 7))
                    sgt = sg[f % 2]
                    sk = "sg%d" % (f % 2)
                    em("act", lambda e, Pg=Pg, sgt=sgt: e.activation(out=sgt[:], in_=Pg, func=AF.Silu), r=[pk], w=[sk])
                    em("dve", lambda e, f=f, Pu=Pu, sgt=sgt: e.tensor_tensor(out=hmT[:, f, :], in0=Pu, in1=sgt[:], op=ALU.mult),
                       r=[pk, sk], w=["hmT"])

            def down_stage(grp):
                for tt in range(4):
                    gt = grp * 4 + tt
                    hbt, hk = hbd[tt % 2], "hbd%d" % (tt % 2)
                    em("sp", lambda e, hbt=hbt, gt=gt: e.dma_start(out=hbt[:], in_=y[gt * 128:(gt + 1) * 128, :]),
                       r=["y%d" % gt], w=[hk], dma=hk)
                    Po = PS[2 + tt % 2]
                    pk = "PQ%d" % (tt % 2)
                    for hf in range(2):
                        for f in range(NFF):
                            em("pe", lambda e, f=f, hf=hf, tt=tt, Po=Po: e.matmul(
                                Po[:, hf * 512:(hf + 1) * 512], lhsT=hmT[:, f, tt * 128:(tt + 1) * 128],
                                rhs=wd[:, f, hf * 512:(hf + 1) * 512], start=(f == 0), stop=(f == NFF - 1)),
                               r=["hmT", "wd"], w=[pk], sig=(f == NFF - 1 and hf == 1))
                    em("dve", lambda e, hbt=hbt, Po=Po: e.tensor_tensor(out=hbt[:], in0=Po[:], in1=hbt[:], op=ALU.add),
                       r=[pk, hk], w=[hk])
                    em("sp", lambda e, hbt=hbt, gt=gt: e.dma_start(out=y[gt * 128:(gt + 1) * 128, :], in_=hbt[:]),
                       r=[hk], w=["y%d" % gt], dma="st" + hk)

            def load_ffn_weights(blocks, with_wd):
                for j in blocks:
                    for c in range(8):
                        em("act", lambda e, c=c, j=j: e.dma_start(out=wg[:, c, j * BW:(j + 1) * BW],
                                                                 in_=wg_bf[c * 128:(c + 1) * 128, j * BW:(j + 1) * BW]),
                           r=["wgbf"], w=["wg%d" % j], dma="wg%d" % j)
                        em("act", lambda e, c=c, j=j: e.dma_start(out=wu[:, c, j * BW:(j + 1) * BW],
                                                                 in_=wu_bf[c * 128:(c + 1) * 128, j * BW:(j + 1) * BW]),
                           r=["wubf"], w=["wu%d" % j], dma="wu%d" % j)
                if with_wd:
                    em("act", lambda e: e.dma_start(out=wd[:], in_=wd_bf.rearrange("(f p) n -> p f n", p=128)),
                       r=["wdbf"], w=["wd"], dma="wd")

            if ngr:
                norm_pre(0)
            load_ffn_weights([0, 1], False)
            if ngr:
                norm_stage(0)
            for grp in range(ngr):
                if grp + 1 < ngr:
                    norm_pre(grp + 1)
                gateup_stage(grp)
                if grp + 1 < ngr:
                    norm_stage(grp + 1)
                down_stage(grp)
            for key, sm in S.dsem.items():
                if key.startswith("st"):
                    nc.sync.wait_ge(sm, S.dcnt[key])
    return nc


def _consts():
    def tables(dim):
        inv = (10000.0 ** (-np.arange(0, dim, 2, dtype=np.float32) / np.float32(dim))).astype(np.float32)
        ang = (np.arange(SEQ, dtype=np.float32)[:, None] * inv[None, :]).astype(np.float32)
        return np.cos(ang).astype(np.float32), np.sin(ang).astype(np.float32)

    cA, sA = tables(64)
    cB, sB = tables(32)
    blk = (np.arange(SEQ)[None, :] // 256 == np.arange(8)[:, None]).astype(np.float32)
    return {"cosA_tab": cA, "sinA_tab": sA, "cosB_tab": cB, "sinB_tab": sB, "blkind": blk}


_NC = None


def kernel(**inputs):
    global _NC
    if _NC is None:
        _NC = build()
    nc = _NC
    x = np.ascontiguousarray(np.asarray(inputs["x"], dtype=np.float32)).reshape(16 * SEQ, DM)
    common = {k: np.ascontiguousarray(np.asarray(v, dtype=np.float32)[0]) for k, v in inputs.items() if k != "x"}
    common.update(_consts())
    in_maps = []
    for c in range(NCORES):
        m = dict(common)
        m["x"] = x[c * TOK:(c + 1) * TOK]
        in_maps.append(m)
    res = run_bass_kernel_spmd(nc, in_maps, core_ids=list(range(NCORES)))
    out = np.concatenate([r["y"] for r in res.results], axis=0)
    return out.reshape(16, SEQ, DM).astype(np.float32)
```

```python
import numpy as np
from contextlib import ExitStack
import concourse.bass as bass
import concourse.mybir as mybir
from concourse.bass_utils import run_bass_kernel_spmd
from concourse.ap import AP

F32 = mybir.dt.float32
BF16 = mybir.dt.bfloat16
ALU = mybir.AluOpType
AF = mybir.ActivationFunctionType
AX = mybir.AxisListType

NCORES = 8
SEQ = 2048
DM = 1024
NSEQ = 2
NT = 16
DFF = 2816
NFF = 22
EPS = 1e-6
NEG = -30000.0
TOK = NSEQ * SEQ


class Sched:
    def __init__(self, nc, es):
        self.nc = nc
        self.es = es
        self.E = {"pe": nc.tensor, "act": nc.scalar, "dve": nc.vector, "pool": nc.gpsimd, "sp": nc.sync}
        self.sem = {k: es.enter_context(nc.semaphore("s_" + k)) for k in self.E}
        self.cnt = {k: 0 for k in self.E}
        self.pending = {k: False for k in self.E}
        self.seen = {k: {} for k in self.E}
        self.lastw = {}
        self.readers = {}
        self.dsem = {}
        self.dcnt = {}
        self.semobj = dict(self.sem)
        self.cnt_misc = 0

    def _dma_sem(self, key):
        if key not in self.dsem:
            self.dsem[key] = self.es.enter_context(self.nc.semaphore("d_" + key))
            self.dcnt[key] = 0
            self.semobj["d_" + key] = self.dsem[key]
        return self.dsem[key]

    def emit(self, eng, fn, r=(), w=(), sig=True, dma=None):
        deps = {}

        def add(tok):
            if tok is None:
                return
            name, val, teng = tok
            if teng == eng and eng == "pe" and not name.startswith("d_"):
                return
            if deps.get(name, 0) < val:
                deps[name] = val

        for k in r:
            add(self.lastw.get(k))
        for k in w:
            add(self.lastw.get(k))
            for t in self.readers.get(k, ()):
                add(t)
        E = self.E[eng]
        for name, val in deps.items():
            if self.seen[eng].get(name, 0) < val:
                E.wait_ge(self.semobj[name], val)
                self.seen[eng][name] = val
        ins = fn(E)
        if dma is not None:
            s = self._dma_sem(dma)
            self.dcnt[dma] += 16
            ins.then_inc(s, 16)
            tok = ("d_" + dma, self.dcnt[dma], eng)
        elif sig:
            self.cnt[eng] += 1
            ins.then_inc(self.sem[eng], 1)
            self.pending[eng] = False
            tok = (eng, self.cnt[eng], eng)
        else:
            self.pending[eng] = True
            tok = (eng, self.cnt[eng] + 1, eng)
        for k in r:
            self.readers.setdefault(k, []).append(tok)
        for k in w:
            self.lastw[k] = tok
            self.readers[k] = []
        return ins

    def barrier(self):
        for e in self.E:
            assert not self.pending[e], e
        for e, E in self.E.items():
            for e2 in self.E:
                if e2 != e and self.seen[e].get(e2, 0) < self.cnt[e2]:
                    E.wait_ge(self.sem[e2], self.cnt[e2])
                    self.seen[e][e2] = self.cnt[e2]
            for key, s in self.dsem.items():
                name = "d_" + key
                if self.seen[e].get(name, 0) < self.dcnt[key]:
                    E.wait_ge(s, self.dcnt[key])
                    self.seen[e][name] = self.dcnt[key]


def fap(ap, *dims, parts=None):
    pp = list(ap.ap[0])
    if parts is not None:
        pp[1] = parts
    return AP(ap.tensor, ap.offset, [pp] + [list(d) for d in dims])


def build(nseq=NSEQ, pairs_a=4, pairs_b=4, phase2=True, ngrp2=TOK // 512):
    nc = bass.Bass("TRN2", target_bir_lowering=False)
    dt = nc.dram_tensor
    x = dt("x", [TOK, DM], F32, kind="ExternalInput").ap()
    attn_g = dt("attn_norm_g", [DM], F32, kind="ExternalInput").ap()
    w_in = dt("w_in", [DM, 1952], F32, kind="ExternalInput").ap()
    mqg = dt("moba_q_norm_g", [64], F32, kind="ExternalInput").ap()
    mkg = dt("moba_k_norm_g", [64], F32, kind="ExternalInput").ap()
    qag = dt("mla_q_a_norm_g", [256], F32, kind="ExternalInput").ap()
    w_qup = dt("w_q_up", [256, 768], F32, kind="ExternalInput").ap()
    kvag = dt("mla_kv_a_norm_g", [128], F32, kind="ExternalInput").ap()
    w_kvup = dt("w_kv_up", [128, 1024], F32, kind="ExternalInput").ap()
    lqg = dt("mla_q_norm_g", [96], F32, kind="ExternalInput").ap()
    lkg = dt("mla_k_norm_g", [96], F32, kind="ExternalInput").ap()
    w_o = dt("w_o", [DM, DM], F32, kind="ExternalInput").ap()
    ffn_g = dt("ffn_norm_g", [DM], F32, kind="ExternalInput").ap()
    w_gate = dt("w_gate", [DM, DFF], F32, kind="ExternalInput").ap()
    w_up = dt("w_up", [DM, DFF], F32, kind="ExternalInput").ap()
    w_down = dt("w_down", [DFF, DM], F32, kind="ExternalInput").ap()
    cosA_d = dt("cosA_tab", [SEQ, 32], F32, kind="ExternalInput").ap()
    sinA_d = dt("sinA_tab", [SEQ, 32], F32, kind="ExternalInput").ap()
    cosB_d = dt("cosB_tab", [SEQ, 16], F32, kind="ExternalInput").ap()
    sinB_d = dt("sinB_tab", [SEQ, 16], F32, kind="ExternalInput").ap()
    blk_d = dt("blkind", [8, SEQ], F32, kind="ExternalInput").ap()
    y = dt("y", [TOK, DM], F32, kind="ExternalOutput").ap()
    lscr = dt("lscr", [8, 512], F32, kind="Internal").ap()
    rscr = dt("rscr", [8, 512], F32, kind="Internal").ap()
    wg_bf = dt("wg_bf", [DM, DFF], BF16, kind="Internal").ap()
    wu_bf = dt("wu_bf", [DM, DFF], BF16, kind="Internal").ap()
    wd_bf = dt("wd_bf", [DFF, DM], BF16, kind="Internal").ap()

    with ExitStack() as es0:
        S = Sched(nc, es0)
        em = S.emit

        def sb(es, name, shape, dtype):
            return es.enter_context(nc.sbuf_tensor(name, shape, dtype))

        PSX = [es0.enter_context(nc.psum_tensor("psx%d" % i, [128, 2048], F32)) for i in range(2)]
        PS = [PSX[i // 2][:, (i % 2) * 1024:(i % 2 + 1) * 1024] for i in range(4)]
        PSb = [p.bitcast(BF16) for p in PS]
        PJ = PSX[0][:].rearrange("p (i n) -> p i n", n=512)
        assert tuple(PSb[1].shape) == (128, 2048), PSb[1].shape

        identf = sb(es0, "identf", [128, 128], F32)
        ident = sb(es0, "ident", [128, 128], BF16)
        gA = sb(es0, "gA", [128, 8], F32)
        gF = sb(es0, "gF", [128, 8], F32)
        epsc = sb(es0, "epsc", [128, 1], F32)
        em("pool", lambda e: e.memset(epsc[:], EPS), w=["epsc"])
        em("pool", lambda e: e.memset(identf[:], 1.0), w=["identf"])
        em("pool", lambda e: e.affine_select(out=identf[:], in_=identf[:], pattern=[[-1, 128]],
                                             compare_op=ALU.is_equal, fill=0.0, base=0, channel_multiplier=1),
           r=["identf"], w=["identf"])
        em("dve", lambda e: e.tensor_copy(out=ident[:], in_=identf[:]), r=["identf"], w=["ident"])

        def rms_rstd(dst, src, n, key):
            em("act", lambda e: e.activation(out=dst, in_=src, func=AF.Ln, scale=1.0 / n, bias=epsc[0:dst.shape[0], 0:1]),
               r=[key, "epsc"], w=[key])
            em("act", lambda e: e.activation(out=dst, in_=dst, func=AF.Exp, scale=-0.5), r=[key], w=[key])

        with ExitStack() as es1:
            wA = sb(es1, "wA", [128, 4, 8, 384], BF16)
            wB = sb(es1, "wB", [128, 8, 416], BF16)
            wq = sb(es1, "wq", [128, 2, 768], BF16)
            wkv = sb(es1, "wkv", [128, 1024], BF16)
            hnT = sb(es1, "hnT", [128, 8, SEQ], BF16)
            wo = hnT[:, 0:4, :].rearrange("p a (b n) -> p (a b) n", b=2)
            cqnT = sb(es1, "cqnT", [128, 2, SEQ], BF16)
            ckvnT = sb(es1, "ckvnT", [128, SEQ], BF16)
            kr = sb(es1, "kr", [128, NT, 32], F32)
            sskpe = sb(es1, "sskpe", [128, NT], F32)
            QT = sb(es1, "QT", [96, 2, SEQ], BF16)
            KT = sb(es1, "KT", [96, 2, SEQ], BF16)
            Vb = sb(es1, "Vb", [128, NT, 192], BF16)
            OT = sb(es1, "OT", [128, 8, SEQ], BF16)
            Pb = [sb(es1, "Pb%d" % i, [128, 1024], BF16) for i in range(3)]
            xbuf = [sb(es1, "xbuf%d" % i, [128, DM], F32) for i in range(2)]
            xs = [sb(es1, "xs%d" % i, [128, DM], BF16) for i in range(2)]
            sqA_full = sb(es1, "sqA", [128, 1280], F32)
            sqA = sqA_full[:, 0:1024]
            sqk = sqA_full[:, 768:1280].rearrange("p (i h d) -> p i h d", i=4, h=2)
            ssA = sb(es1, "ssA", [128, 8], F32)
            xn = xbuf
            rt = [sb(es1, "rt%d" % i, [128, 512], F32) for i in range(4)]
            qkb = [sb(es1, "qkb%d" % i, [128, 4, 4, 96], BF16) for i in range(2)]
            qtok = sb(es1, "qtok", [128, 8, 2, 72], BF16)
            tabA = sb(es1, "tabA", [128, NT, 2, 4, 32], F32)
            cosB = sb(es1, "cosB", [128, NT, 16], F32)
            sinB = sb(es1, "sinB", [128, NT, 16], F32)
            gqkA = sb(es1, "gqkA", [128, 2, 64], F32)
            gropeB = sb(es1, "gropeB", [128, 2, 32], F32)
            gvec = sb(es1, "gvec", [96, 2], F32)
            gqa = sb(es1, "gqa", [128, 2], F32)
            gkva = sb(es1, "gkva", [128, 1], F32)
            trim = sb(es1, "trim", [128, 128], BF16)
            gsb = sb(es1, "gsb", [128, 128], F32)
            trimf = gsb
            NB_ = 3
            Osb = [sb(es1, "Osb%d" % i, [128, 512], F32) for i in range(NB_)]
            lcol = [sb(es1, "lcol%d" % i, [128, 4], F32) for i in range(NB_)]
            Lsb = [sb(es1, "Lsb%d" % i, [128, 512], F32) for i in range(NB_)]
            print("phase1 sbuf remaining", nc.sbuf_bytes_remaining)
            kmf = sb(es1, "kmf", [64, 2, 8], F32)
            kmb = sb(es1, "kmb", [64, 2, 8], BF16)
            cmp_t = sqA_full[:, 0:256].rearrange("p (a b c) -> p a b c", a=4, b=8)
            cnt_t = sqA_full[:, 256:288].rearrange("p (a b) -> p a b", a=4)
            cqb = qkb[1][:, 0, :, :].rearrange("p a b -> p (a b)")
            kpg = sb(es1, "kpg", [128, 32], F32)
            ssq = sb(es1, "ssq", [128, 16], F32)

            for t in range(2):
                em("sp", lambda e, t=t: e.dma_start(out=xbuf[t][:], in_=x[t * 128:(t + 1) * 128, :]),
                   w=["xbuf%d" % t], dma="xbuf%d" % t)
            em("sp", lambda e: e.dma_start(out=gA[:], in_=attn_g.rearrange("(c p) -> p c", p=128),
                                           allow_slow_non_contiguous=True), w=["gA"], dma="gA")
            em("sp", lambda e: e.dma_start(out=gF[:], in_=ffn_g.rearrange("(c p) -> p c", p=128),
                                           allow_slow_non_contiguous=True), w=["gF"], dma="gF")
            for (dst, src, key) in ((cosB, cosB_d, "cosB"), (sinB, sinB_d, "sinB")):
                em("sp", lambda e, dst=dst, src=src: e.dma_start(
                    out=dst[:], in_=src.rearrange("(t p) d -> p t d", p=128)), w=[key], dma=key)
            cosA = Osb[0][:].rearrange("p (t d) -> p t d", d=32)
            sinA = Osb[1][:].rearrange("p (t d) -> p t d", d=32)
            em("sp", lambda e: e.dma_start(out=cosA, in_=cosA_d.rearrange("(t p) d -> p t d", p=128)), w=["Osb0"], dma="cosA")
            em("sp", lambda e: e.dma_start(out=sinA, in_=sinA_d.rearrange("(t p) d -> p t d", p=128)), w=["Osb1"], dma="sinA")
            em("sp", lambda e: e.dma_start(out=gqkA[:, 0, :], in_=mqg.partition_broadcast(128)), w=["gqkA"], dma="gqkA")
            em("sp", lambda e: e.dma_start(out=gqkA[:, 1, :], in_=mkg.partition_broadcast(128)), w=["gqkA"], dma="gqkA")
            em("sp", lambda e: e.dma_start(out=gropeB[:, 0, :], in_=lqg[64:96].partition_broadcast(128)),
               w=["gropeB"], dma="gropeB")
            em("sp", lambda e: e.dma_start(out=gropeB[:, 1, :], in_=lkg[64:96].partition_broadcast(128)),
               w=["gropeB"], dma="gropeB")
            em("pool", lambda e: e.memset(gvec[:], 1.0), w=["gvec"])
            em("sp", lambda e: e.dma_start(out=gvec[0:64, 0:1], in_=lqg[0:64].rearrange("(p o) -> p o", o=1)),
               w=["gvec"], dma="gvec")
            em("sp", lambda e: e.dma_start(out=gvec[0:64, 1:2], in_=lkg[0:64].rearrange("(p o) -> p o", o=1)),
               w=["gvec"], dma="gvec")
            em("sp", lambda e: e.dma_start(out=gqa[:], in_=qag.rearrange("(c p) -> p c", p=128),
                                           allow_slow_non_contiguous=True), w=["gqa"], dma="gqa")
            em("sp", lambda e: e.dma_start(out=gkva[:], in_=kvag.rearrange("(p o) -> p o", o=1)),
               w=["gkva"], dma="gkva")
            em("pool", lambda e: e.memset(trimf[:], 0.0), w=["gsb"])
            em("pool", lambda e: e.affine_select(out=trimf[:], in_=trimf[:], pattern=[[1, 128]],
                                                 compare_op=ALU.is_ge, fill=NEG, base=0, channel_multiplier=-1),
               r=["gsb"], w=["gsb"])
            em("dve", lambda e: e.tensor_copy(out=trim[:], in_=trimf[:]), r=["gsb"], w=["trim"])
            em("pool", lambda e: e.memset(Vb[:, :, 64:128], 0.0), w=["Vb"])
            em("pool", lambda e: e.memset(Vb[:, :, 64:65], 1.0), w=["Vb"])

            for p in range(4):
                for j in range(3):
                    c0 = j * 512 + p * 128
                    em("pool", lambda e, p=p, j=j, c0=c0: e.dma_start(
                        out=wA[:, p, :, j * 128:(j + 1) * 128],
                        in_=w_in[:, c0:c0 + 128].rearrange("(c q) n -> q c n", q=128)),
                       w=["wA%d" % p], dma="wA%d" % p)
            em("pool", lambda e: e.dma_start(out=wB[:], in_=w_in[:, 1536:1952].rearrange("(c q) n -> q c n", q=128)),
               w=["wB"], dma="wB")
            em("pool", lambda e: e.dma_start(out=wq[:], in_=w_qup.rearrange("(c q) n -> q c n", q=128)),
               w=["wq"], dma="wq")
            em("pool", lambda e: e.dma_start(out=wkv[:], in_=w_kvup), w=["wkv"], dma="wkv")
            def build_tabA():
                for a_ in range(2):
                    for kind, (trig, tk, h0) in enumerate(((cosA, "Osb0", 0), (sinA, "Osb1", 32), (cosA, "Osb0", 32), (sinA, "Osb1", 0))):
                        em("dve", lambda e, a_=a_, kind=kind, trig=trig, h0=h0: e.tensor_tensor(
                            out=tabA[:, :, a_, kind, :], in0=trig, in1=fap(gqkA[:, a_, h0:h0 + 1], [0, NT], [1, 32]), op=ALU.mult),
                           r=[tk, "gqkA"], w=["tabA"])

            if pairs_a < 4 or pairs_b < 4:
                em("pool", lambda e: e.memset(OT[:], 0.0), w=["OT"])
            cosBb = lambda t, n: fap(cosB[:, t, 0:1], [0, n], [1, 16])
            sinBb = lambda t, n: fap(sinB[:, t, 0:1], [0, n], [1, 16])

            def rope(src_x1, src_x2, cb, sb_, dst1, dst2, tmp, rkeys, wkeys, tabkeys):
                t1, t2 = tmp
                em("pool", lambda e: e.tensor_tensor(out=t1, in0=src_x1, in1=cb, op=ALU.mult), r=rkeys + tabkeys, w=["rt0"])
                em("pool", lambda e: e.tensor_tensor(out=t2, in0=src_x2, in1=sb_, op=ALU.mult), r=rkeys + tabkeys, w=["rt1"])
                em("pool", lambda e: e.tensor_tensor(out=dst1, in0=t1, in1=t2, op=ALU.subtract), r=["rt0", "rt1"], w=wkeys)
                em("pool", lambda e: e.tensor_tensor(out=t1, in0=src_x2, in1=cb, op=ALU.mult), r=rkeys + tabkeys, w=["rt0"])
                em("pool", lambda e: e.tensor_tensor(out=t2, in0=src_x1, in1=sb_, op=ALU.mult), r=rkeys + tabkeys, w=["rt1"])
                em("pool", lambda e: e.tensor_tensor(out=dst2, in0=t1, in1=t2, op=ALU.add), r=["rt0", "rt1"], w=wkeys)

            norm_pending = []

            def norm_tail(k, rows, chunk, g):
                osb, lsb, lc = Osb[k], Lsb[k], lcol[k]
                okey2, lkey2, ckey = "Osb%d" % k, "Lsb%d" % k, "lcol%d" % k
                em("dve", lambda e: e.reciprocal(out=lc[:], in_=lc[:]), r=[ckey], w=[ckey])
                em("sp", lambda e: e.dma_start(out=rscr[k, :].rearrange("(p f) -> p f", f=4), in_=lc[:]),
                   r=[ckey], w=["rscr%d" % k], dma="ls%d" % k)
                em("sp", lambda e: e.dma_start(out=lsb[rows, :], in_=rscr[k, :].partition_broadcast(64)),
                   r=["rscr%d" % k], w=[lkey2], dma="ls%d" % k)
                em("pool", lambda e: e.tensor_tensor(out=OT[rows, chunk, g * 512:(g + 1) * 512],
                                                     in0=osb[rows, :], in1=lsb[rows, :], op=ALU.mult),
                   r=[okey2, lkey2], w=["OT"])

            def norm_flush():
                while norm_pending:
                    norm_tail(*norm_pending.pop(0))

            def attention(R, scale, chunk):
                skeys = lambda k: ["PL", "PM"] if k == "PL" else [k]
                jobs = []
                for hh in range(2):
                    for g in range(4):
                        nk = 4 * g + 4
                        for j in range(nk // 2):
                            jobs.append((hh, g, j, j == nk // 2 - 1))

                def s_stage(idx):
                    hh, g, j, last = jobs[idx]
                    Sb = [PS[0], PS[1], PS[3]][idx % 3]
                    skey = ["PS0", "PS1", "PL"][idx % 3]
                    for u in range(2):
                        kt = 2 * j + u
                        c0 = max(0, kt - 4 * g) * 128
                        diag = kt >= 4 * g
                        em("pe", lambda e, u=u, kt=kt, c0=c0, diag=diag: e.matmul(
                            Sb[:, u * 512 + c0:(u + 1) * 512],
                            lhsT=KT[0:R, hh, kt * 128:(kt + 1) * 128],
                            rhs=QT[0:R, hh, g * 512 + c0:(g + 1) * 512],
                            start=True, stop=not diag),
                           r=["QT", "KT"], w=skeys(skey), sig=(u == 1 and not diag))
                        if diag:
                            em("pe", lambda e, u=u, c0=c0: e.matmul(
                                Sb[:, u * 512 + c0:u * 512 + c0 + 128], lhsT=ident[:], rhs=trim[:],
                                start=False, stop=True),
                               r=["ident", "trim"], w=skeys(skey), sig=(u == 1))
                    kt0 = 2 * j
                    cmin = max(0, kt0 - 4 * g) * 128
                    Pt = Pb[idx % 3]
                    pkey = "Pb%d" % (idx % 3)
                    em("act", lambda e: e.activation(
                        out=fap(Pt[:, cmin:cmin + 1], [512, 2], [1, 512 - cmin]),
                        in_=fap(Sb[:, cmin:cmin + 1], [512, 2], [1, 512 - cmin]),
                        func=AF.Exp, scale=scale), r=skeys(skey), w=[pkey])

                def pv_stage(idx):
                    hh, g, j, last = jobs[idx]
                    rows = slice(0, 64) if hh == 0 else slice(64, 128)
                    lp = 64 if hh == 0 else 0
                    vsl = slice(0, 65) if hh == 0 else slice(64, 192)
                    orows = 65 if hh == 0 else 128
                    Pt = Pb[idx % 3]
                    pkey = "Pb%d" % (idx % 3)
                    ob = g % 2
                    Ob = PS[2][:, ob * 512:(ob + 1) * 512]
                    okey = "PO%d" % ob
                    for u in range(2):
                        kt = 2 * j + u
                        c0 = max(0, kt - 4 * g) * 128
                        fin = last and u == 1
                        em("pe", lambda e, u=u, kt=kt, c0=c0: e.matmul(
                            Ob[0:orows, c0:512], lhsT=Vb[:, kt, vsl], rhs=Pt[:, u * 512 + c0:(u + 1) * 512],
                            start=(kt == 0), stop=(kt == 4 * g + 3)),
                           r=[pkey, "Vb"], w=[okey], sig=fin)
                    if last:
                        k = S.cnt_misc % NB_
                        S.cnt_misc += 1
                        osb, lc = Osb[k], lcol[k]
                        okey2, ckey = "Osb%d" % k, "lcol%d" % k
                        em("dve", lambda e: e.tensor_copy(out=osb[0:orows, :], in_=Ob[0:orows, :]), r=[okey], w=[okey2])
                        em("sp", lambda e: e.dma_start(out=lscr[k:k + 1, :], in_=osb[lp:lp + 1, :]),
                           r=[okey2], w=["lscr%d" % k], dma="ls%d" % k)
                        em("sp", lambda e: e.dma_start(out=lc[:], in_=lscr[k, :].rearrange("(p f) -> p f", f=4)),
                           r=["lscr%d" % k], w=[ckey], dma="ls%d" % k)
                        norm_pending.append((k, rows, chunk, g))
                        while len(norm_pending) > 1:
                            norm_tail(*norm_pending.pop(0))

                n = len(jobs)
                s_stage(0)
                s_stage(1)
                for idx in range(n):
                    if idx + 2 < n:
                        s_stage(idx + 2)
                    pv_stage(idx)

            conv_jobs = []
            for q4 in range(4):
                conv_jobs.append((wg_bf[q4 * 256:(q4 + 1) * 256, :], w_gate[q4 * 256:(q4 + 1) * 256, :], "wgbf"))
                conv_jobs.append((wu_bf[q4 * 256:(q4 + 1) * 256, :], w_up[q4 * 256:(q4 + 1) * 256, :], "wubf"))
            for q4 in range(4):
                conv_jobs.append((wd_bf[q4 * 704:(q4 + 1) * 704, :], w_down[q4 * 704:(q4 + 1) * 704, :], "wdbf"))

            def conv_step():
                if conv_jobs:
                    dst, src, key = conv_jobs.pop(0)
                    em("pool", lambda e: e.dma_start(out=dst, in_=src), w=[key], dma=key)

            for s in range(nseq):
                tok0 = s * SEQ
                def A1(t):
                    xb = xbuf[t % 2]
                    xk = "xbuf%d" % (t % 2)
                    col = ssA[:, 3 + t % 4:4 + t % 4]
                    if not (s == 0 and t < 2):
                        em("sp", lambda e: e.dma_start(out=xb[:], in_=x[tok0 + t * 128: tok0 + (t + 1) * 128, :]),
                           w=[xk], dma=xk)
                    em("act", lambda e: e.activation(out=xs[t % 2][:], in_=xb[:], func=AF.Square, accum_out=col),
                       r=[xk], w=["xs%d" % (t % 2), "ssA%d" % (t % 4)])
                    rms_rstd(col, col, DM, "ssA%d" % (t % 4))

                def A2(t):
                    xb = xbuf[t % 2]
                    xk = "xbuf%d" % (t % 2)
                    col = ssA[:, 3 + t % 4:4 + t % 4]
                    xsb = xs[t % 2]
                    xsk = "xs%d" % (t % 2)
                    em("dve", lambda e: e.tensor_scalar(out=xsb[:], in0=xb[:], scalar1=col, scalar2=None, op0=ALU.mult),
                       r=[xk, "ssA%d" % (t % 4)], w=[xsk])
                    tt = t % 4
                    for c in range(8):
                        em("pe", lambda e, c=c: e.transpose(
                            out=PSb[c // 4][:, (c % 4) * 512 + tt * 128:(c % 4) * 512 + (tt + 1) * 128],
                            in_=xsb[:, c * 128:(c + 1) * 128], identity=ident[:]),
                           r=[xsk, "ident"], w=["PS%d" % (c // 4)], sig=(c == 7))
                    if tt == 3:
                        g0 = (t // 4) * 512
                        for c in range(8):
                            em("dve", lambda e, c=c: e.tensor_scalar(
                                out=hnT[:, c, g0:g0 + 512], in0=PSb[c // 4][:, (c % 4) * 512:(c % 4 + 1) * 512],
                                scalar1=gA[:, c:c + 1], scalar2=None, op0=ALU.mult),
                               r=["PS%d" % (c // 4), "gA"], w=["hnT"])

                for i in range(NT + 1):
                    if i < NT:
                        A1(i)
                    if i >= 1:
                        A2(i - 1)
                if s == 0:
                    build_tabA()

                def B_pe(t):
                        pb = t % 2
                        Pp = PS[2][:, pb * 512:pb * 512 + 416]
                        pk = "PO%d" % pb
                        for c in range(8):
                            em("pe", lambda e, c=c, t=t, Pp=Pp: e.matmul(
                                Pp, lhsT=hnT[:, c, t * 128:(t + 1) * 128], rhs=wB[:, c, :], start=(c == 0), stop=(c == 7)),
                               r=["hnT", "wB"], w=[pk], sig=(c == 7))

                def B_mid(t):
                        pb = t % 2
                        Pp = PS[2][:, pb * 512:pb * 512 + 416]
                        pk = "PO%d" % pb
                        em("act", lambda e, Pp=Pp: e.activation(out=sqA[:, 0:416], in_=Pp, func=AF.Square), r=[pk], w=["sqA"])
                        em("dve", lambda e: e.reduce_sum(out=ssA[:, 1:2], in_=sqA[:, 0:256], axis=AX.X), r=["sqA"], w=["ssB"])
                        em("dve", lambda e: e.reduce_sum(out=ssA[:, 2:3], in_=sqA[:, 256:384], axis=AX.X), r=["sqA"], w=["ssB"])
                        em("dve", lambda e, t=t: e.reduce_sum(out=sskpe[:, t:t + 1], in_=sqA[:, 384:416], axis=AX.X),
                           r=["sqA"], w=["sskpe"])
                        rms_rstd(ssA[:, 1:2], ssA[:, 1:2], 256, "ssB")
                        rms_rstd(ssA[:, 2:3], ssA[:, 2:3], 128, "ssB")
                        em("dve", lambda e, Pp=Pp: e.tensor_scalar(out=cqb[:, 0:256], in0=Pp[:, 0:256], scalar1=ssA[:, 1:2],
                                                                  scalar2=None, op0=ALU.mult), r=[pk, "ssB"], w=["qkb1"])
                        em("dve", lambda e, Pp=Pp: e.tensor_scalar(out=cqb[:, 256:384], in0=Pp[:, 256:384], scalar1=ssA[:, 2:3],
                                                                  scalar2=None, op0=ALU.mult), r=[pk, "ssB"], w=["qkb1"])
                        em("dve", lambda e, Pp=Pp: e.tensor_tensor(out=kpg[:], in0=Pp[:, 384:416], in1=gropeB[:, 1, :], op=ALU.mult),
                           r=[pk, "gropeB"], w=["kpg"])
                        rope(kpg[:, 0:16], kpg[:, 16:32], cosB[:, t, :], sinB[:, t, :],
                             kr[:, t, 0:16], kr[:, t, 16:32],
                             [rt[i][:, 0:16] for i in range(2)], ["kpg"], ["kr"], ["cosB", "sinB"])

                def B_tr(t):
                        PT = PSb[3][:, 1024:1024 + 384]
                        for c in range(3):
                            em("pe", lambda e, c=c, PT=PT: e.transpose(out=PT[:, c * 128:(c + 1) * 128],
                                                                      in_=cqb[:, c * 128:(c + 1) * 128], identity=ident[:]),
                               r=["qkb1", "ident"], w=["PM"], sig=(c == 2))
                        for c in range(2):
                            em("dve", lambda e, c=c, t=t, PT=PT: e.tensor_scalar(
                                out=cqnT[:, c, t * 128:(t + 1) * 128], in0=PT[:, c * 128:(c + 1) * 128],
                                scalar1=gqa[:, c:c + 1], scalar2=None, op0=ALU.mult), r=["PM", "gqa"], w=["cqnT"])
                        em("dve", lambda e, t=t, PT=PT: e.tensor_scalar(
                            out=ckvnT[:, t * 128:(t + 1) * 128], in0=PT[:, 256:384],
                            scalar1=gkva[:, 0:1], scalar2=None, op0=ALU.mult), r=["PM", "gkva"], w=["ckvnT"])

                B_pe(0)
                for t in range(NT):
                    if t + 1 < NT:
                        B_pe(t + 1)
                    B_mid(t)
                    B_tr(t)

                em("pool", lambda e: e.dma_start(out=KT[64:72, 0, :], in_=blk_d), w=["KT"], dma="blk")
                em("pool", lambda e: e.dma_start(out=KT[64:72, 1, :], in_=blk_d), w=["KT"], dma="blk")
                em("pool", lambda e: e.memset(QT[64:72, :, :], 0.0), w=["QT"])
                for p in range(pairs_a):
                    def moba_s1(b, p=p):
                        for i in range(4):
                            t = 4 * b + i
                            for c in range(8):
                                em("pe", lambda e, c=c, t=t, i=i: e.matmul(
                                    PJ[:, i, 0:384], lhsT=hnT[:, c, t * 128:(t + 1) * 128], rhs=wA[:, p, c, :],
                                    start=(c == 0), stop=(c == 7)),
                                   r=["hnT", "wA%d" % p], w=["PS%d" % (i // 2)], sig=(c == 7))

                    def moba_s1b(b, p=p):
                        pk = ["PS0", "PS1"]
                        xnb = xn[b % 2]
                        xnk = "xbuf%d" % (b % 2)
                        em("act", lambda e: e.activation(out=sqA[:].rearrange("p (i n) -> p i n", n=256),
                                                         in_=PJ[:, :, 0:256], func=AF.Square), r=pk, w=["sqA"])
                        em("dve", lambda e: e.reduce_sum(out=ssq[:], in_=sqA[:].rearrange("p (h d) -> p h d", d=64),
                                                         axis=AX.X), r=["sqA"], w=["ssq"])
                        rms_rstd(ssq[:], ssq[:], 64, "ssq")
                        em("dve", lambda e: e.tensor_tensor(
                            out=fap(xnb[:, 0:1], [256, 4], [64, 4], [1, 64]),
                            in0=fap(PJ[:, 0, 0:1], [512, 4], [64, 4], [1, 64]),
                            in1=fap(ssq[:, 0:1], [4, 4], [1, 4], [0, 64]), op=ALU.mult), r=pk + ["ssq"], w=[xnk])
                        em("act", lambda e: e.activation(
                            out=fap(Vb[:, 4 * b, 0:1], [192, 4], [128, 2], [1, 64]),
                            in_=fap(PJ[:, 0, 256:257], [512, 4], [64, 2], [1, 64]), func=AF.Copy),
                           r=pk, w=["Vb"])

                    def moba_s2(b, p=p):
                        xnb = xn[b % 2]
                        xnk = "xbuf%d" % (b % 2)
                        qb = qkb[b % 2]
                        qk_ = "qkb%d" % (b % 2)
                        x1 = fap(xnb[:, 0:1], [256, 4], [128, 2], [64, 2], [1, 32])
                        x2 = fap(xnb[:, 32:33], [256, 4], [128, 2], [64, 2], [1, 32])
                        tab = lambda kind: fap(tabA[:, 4 * b, 0, kind, 0:1], [256, 4], [128, 2], [0, 2], [1, 32])
                        tv = [rt[i][:].rearrange("p (i a h d) -> p i a h d", i=4, a=2, h=2) for i in range(4)]
                        d1 = fap(qb[:, 0, 0, 0:1], [384, 4], [192, 2], [96, 2], [1, 32])
                        d2 = fap(qb[:, 0, 0, 32:33], [384, 4], [192, 2], [96, 2], [1, 32])
                        em("pool", lambda e: e.tensor_tensor(out=tv[0], in0=x1, in1=tab(0), op=ALU.mult), r=[xnk, "tabA"], w=["rt0"])
                        em("pool", lambda e: e.tensor_tensor(out=tv[1], in0=x2, in1=tab(1), op=ALU.mult), r=[xnk, "tabA"], w=["rt1"])
                        em("pool", lambda e: e.tensor_tensor(out=d1, in0=tv[0], in1=tv[1], op=ALU.subtract), r=["rt0", "rt1"], w=[qk_])
                        em("dve", lambda e: e.tensor_tensor(out=tv[2], in0=x2, in1=tab(2), op=ALU.mult), r=[xnk, "tabA"], w=["rt2"])
                        em("dve", lambda e: e.tensor_tensor(out=tv[3], in0=x1, in1=tab(3), op=ALU.mult), r=[xnk, "tabA"], w=["rt3"])
                        em("dve", lambda e: e.tensor_tensor(out=d2, in0=tv[2], in1=tv[3], op=ALU.add), r=["rt2", "rt3"], w=[qk_])

                    def moba_kmean(b):
                        em("dve", lambda e: e.reduce_sum(
                            out=kmf[:, :, 2 * b:2 * b + 2],
                            in_=KT[0:64, :, 4 * b * 128:(4 * b + 4) * 128].rearrange("p h (b k) -> p h b k", k=256),
                            axis=AX.X), r=["KT"], w=["kmf"])

                    def moba_s2b(b, p=p):
                        qb = qkb[b % 2]
                        qk_ = "qkb%d" % (b % 2)
                        if b >= 2:
                            em("act", lambda e: e.activation(out=qtok[:, 4 * (b - 2):4 * (b - 2) + 4, :, 0:64],
                                                             in_=qb[:, :, 0:2, 0:64], func=AF.Copy), r=[qk_], w=["qtok"])
                        PT = PSb[3].rearrange("p (h k) -> p h k", k=128)
                        for i in range(4):
                            for j in range(4):
                                em("pe", lambda e, i=i, j=j: e.transpose(out=PT[0:64, i * 4 + j, :], in_=qb[:, i, j, 0:64],
                                                                         identity=ident[:]),
                                   r=[qk_, "ident"], w=["PL", "PM"], sig=(i == 3 and j == 3))
                        em("act", lambda e: e.activation(
                            out=fap(QT[0:64, 0, 4 * b * 128:4 * b * 128 + 1], [SEQ, 2], [128, 4], [1, 128]),
                            in_=fap(PT[0:64, 0, 0:1], [128, 2], [512, 4], [1, 128]), func=AF.Copy), r=["PL", "PM"], w=["QT"])
                        em("act", lambda e: e.activation(
                            out=fap(KT[0:64, 0, 4 * b * 128:4 * b * 128 + 1], [SEQ, 2], [128, 4], [1, 128]),
                            in_=fap(PT[0:64, 2, 0:1], [128, 2], [512, 4], [1, 128]), func=AF.Copy), r=["PL", "PM"], w=["KT"])

                    for b in range(5):
                        if b < 4:
                            moba_s1(b)
                        if b >= 1:
                            moba_s2(b - 1)
                        if b >= 2:
                            moba_kmean(b - 2)
                        if b < 4:
                            moba_s1b(b)
                        if b >= 1:
                            moba_s2b(b - 1)
                    moba_kmean(3)
                    em("dve", lambda e: e.tensor_scalar(out=kmb[:], in0=kmf[:], scalar1=1.0 / 256, scalar2=None, op0=ALU.mult),
                       r=["kmf"], w=["kmb"])
                    Gp = PS[3][:, 0:128]
                    for tl in range(8):
                        for hh in range(2):
                            em("pe", lambda e, tl=tl, hh=hh: e.matmul(
                                Gp[:, (tl * 2 + hh) * 8:(tl * 2 + hh) * 8 + 8],
                                lhsT=QT[0:64, hh, (8 + tl) * 128:(9 + tl) * 128], rhs=kmb[:, hh, :], start=True, stop=True),
                               r=["QT", "kmb"], w=["PL"], sig=(tl == 7 and hh == 1))
                    em("dve", lambda e: e.tensor_copy(out=gsb[:], in_=Gp), r=["PL"], w=["gsb"])
                    em("pool", lambda e: e.memset(qtok[:, :, :, 64:72], 0.0), w=["qtok"])
                    for b in range(4, 8):
                        tl = 2 * b - 8
                        base = gsb[:, tl * 16:tl * 16 + 1]
                        em("dve", lambda e, b=b, base=base: e.tensor_tensor(
                            out=cmp_t[:, :, 0:b, 0:b], in0=fap(base, [8, 4], [0, b], [1, b]),
                            in1=fap(base, [8, 4], [1, b], [0, b]), op=ALU.is_gt), r=["gsb"], w=["sqA"])
                        em("dve", lambda e, b=b: e.reduce_sum(out=cnt_t[:, :, 0:b], in_=cmp_t[:, :, 0:b, 0:b], axis=AX.X),
                           r=["sqA"], w=["sqA"])
                        em("dve", lambda e, b=b, tl=tl: e.tensor_scalar(
                            out=fap(qtok[:, tl, 0, 64:65], [72, 4], [1, b]), in0=cnt_t[:, :, 0:b],
                            scalar1=2.5, scalar2=NEG, op0=ALU.is_ge, op1=ALU.mult), r=["sqA"], w=["qtok"])
                    for half in range(2):
                        PT2 = PSb[half][:, 0:1024].rearrange("p (i k) -> p i k", k=128)
                        for i in range(8):
                            tl, hh = half * 4 + i // 2, i % 2
                            em("pe", lambda e, i=i, tl=tl, hh=hh, PT2=PT2: e.transpose(
                                out=PT2[0:72, i, :], in_=qtok[:, tl, hh, :], identity=ident[:]),
                               r=["qtok", "ident"], w=["PS%d" % half], sig=(i == 7))
                        for hh in range(2):
                            em("dve", lambda e, hh=hh, half=half, PT2=PT2: e.tensor_copy(
                                out=QT[64:72, hh, 1024 + half * 512:1024 + (half + 1) * 512].rearrange("p (t k) -> p t k", k=128),
                                in_=fap(PT2[64:72, hh, 0:1], [256, 4], [1, 128])), r=["PS%d" % half], w=["QT"])
                    conv_step()
                    attention(72, 0.125, p)

                sc_mla = 96.0 ** -0.5
                em("pool", lambda e: e.dma_start(out=wo, in_=w_o.rearrange("(c q) n -> q c n", q=128)),
                   w=["hnT"], dma="wo")
                for p in range(pairs_b):
                    def mla_s1(b, p=p):
                        for i in range(4):
                            t = 4 * b + i
                            for c in range(2):
                                em("pe", lambda e, c=c, t=t, i=i: e.matmul(
                                    PJ[:, i, 0:192], lhsT=cqnT[:, c, t * 128:(t + 1) * 128], rhs=wq[:, c, p * 192:(p + 1) * 192],
                                    start=(c == 0), stop=(c == 1)), r=["cqnT", "wq"], w=["PS%d" % (i // 2)], sig=False)
                            em("pe", lambda e, t=t, i=i: e.matmul(
                                PJ[:, i, 256:512], lhsT=ckvnT[:, t * 128:(t + 1) * 128], rhs=wkv[:, p * 256:(p + 1) * 256],
                                start=True, stop=True), r=["ckvnT", "wkv"], w=["PS%d" % (i // 2)], sig=True)

                    def mla_s1b(b, p=p):
                        pk = ["PS0", "PS1"]
                        xnb = xn[b % 2]
                        xnk = "xbuf%d" % (b % 2)
                        qb = qkb[b % 2]
                        qk_ = "qkb%d" % (b % 2)
                        knope = fap(PJ[:, 0, 256:257], [512, 4], [128, 2], [1, 64])
                        em("act", lambda e: e.activation(out=sqA[:, 0:768].rearrange("p (i n) -> p i n", n=192),
                                                         in_=PJ[:, :, 0:192], func=AF.Square), r=pk, w=["sqA"])
                        em("act", lambda e: e.activation(out=sqk, in_=knope, func=AF.Square), r=pk, w=["sqA"])
                        em("dve", lambda e: e.reduce_sum(out=ssq[:, 0:8], in_=sqA[:, 0:768].rearrange("p (h d) -> p h d", d=96),
                                                         axis=AX.X), r=["sqA"], w=["ssq"])
                        em("dve", lambda e: e.reduce_sum(out=ssq[:, 8:16], in_=sqk.rearrange("p i h d -> p (i h) d"),
                                                         axis=AX.X), r=["sqA"], w=["ssq"])
                        em("dve", lambda e: e.tensor_tensor(
                            out=ssq[:, 8:16].rearrange("p (i h) -> p i h", h=2), in0=ssq[:, 8:16].rearrange("p (i h) -> p i h", h=2),
                            in1=fap(sskpe[:, 4 * b:4 * b + 1], [1, 4], [0, 2]), op=ALU.add), r=["ssq", "sskpe"], w=["ssq"])
                        rms_rstd(ssq[:], ssq[:], 96, "ssq")
                        em("dve", lambda e: e.tensor_tensor(
                            out=fap(xnb[:, 0:1], [192, 4], [96, 2], [1, 96]),
                            in0=fap(PJ[:, 0, 0:1], [512, 4], [96, 2], [1, 96]),
                            in1=fap(ssq[:, 0:1], [2, 4], [1, 2], [0, 96]), op=ALU.mult), r=pk + ["ssq"], w=[xnk])
                        em("dve", lambda e: e.tensor_tensor(
                            out=qb[:, :, 2:4, 0:64], in0=knope, in1=fap(ssq[:, 8:9], [2, 4], [1, 2], [0, 64]), op=ALU.mult),
                           r=pk + ["ssq"], w=[qk_])
                        em("dve", lambda e: e.tensor_tensor(
                            out=qb[:, :, 2:4, 64:96], in0=fap(kr[:, 4 * b, 0:1], [32, 4], [0, 2], [1, 32]),
                            in1=fap(ssq[:, 8:9], [2, 4], [1, 2], [0, 32]), op=ALU.mult), r=["kr", "ssq"], w=[qk_])
                        em("act", lambda e: e.activation(
                            out=fap(Vb[:, 4 * b, 0:1], [192, 4], [128, 2], [1, 64]),
                            in_=fap(PJ[:, 0, 320:321], [512, 4], [128, 2], [1, 64]), func=AF.Copy), r=pk, w=["Vb"])

                    def mla_s2(b, p=p):
                        xnb = xn[b % 2]
                        xnk = "xbuf%d" % (b % 2)
                        qb = qkb[b % 2]
                        qk_ = "qkb%d" % (b % 2)
                        em("act", lambda e: e.activation(out=qb[:, :, 0:2, 0:64],
                                                         in_=fap(xnb[:, 0:1], [192, 4], [96, 2], [1, 64]), func=AF.Copy),
                           r=[xnk], w=[qk_])
                        em("pool", lambda e: e.tensor_tensor(
                            out=fap(xnb[:, 64:65], [192, 4], [96, 2], [1, 32]), in0=fap(xnb[:, 64:65], [192, 4], [96, 2], [1, 32]),
                            in1=fap(gropeB[:, 0, 0:1], [0, 4], [0, 2], [1, 32]), op=ALU.mult), r=[xnk, "gropeB"], w=[xnk])
                        x1 = fap(xnb[:, 64:65], [192, 4], [96, 2], [1, 16])
                        x2 = fap(xnb[:, 80:81], [192, 4], [96, 2], [1, 16])
                        cb = fap(cosB[:, 4 * b, 0:1], [16, 4], [0, 2], [1, 16])
                        sb_ = fap(sinB[:, 4 * b, 0:1], [16, 4], [0, 2], [1, 16])
                        tmp = [rt[i][:, 0:128].rearrange("p (i h d) -> p i h d", i=4, h=2) for i in range(2)]
                        rope(x1, x2, cb, sb_, qb[:, :, 0:2, 64:80], qb[:, :, 0:2, 80:96], tmp, [xnk], [qk_], ["cosB", "sinB"])

                    def mla_s2b(b, p=p):
                        qb = qkb[b % 2]
                        qk_ = "qkb%d" % (b % 2)
                        PT = PSb[3].rearrange("p (h k) -> p h k", k=128)
                        for i in range(4):
                            for j in range(4):
                                em("pe", lambda e, i=i, j=j: e.transpose(out=PT[0:96, i * 4 + j, :], in_=qb[:, i, j, :],
                                                                         identity=ident[:]),
                                   r=[qk_, "ident"], w=["PL", "PM"], sig=(i == 3 and j == 3))
                        em("act", lambda e: e.activation(
                            out=fap(QT[0:96, 0, 4 * b * 128:4 * b * 128 + 1], [SEQ, 2], [128, 4], [1, 128]),
                            in_=fap(PT[0:96, 0, 0:1], [128, 2], [512, 4], [1, 128]),
                            func=AF.Copy, scale=gvec[:, 0:1]), r=["PL", "PM", "gvec"], w=["QT"])
                        em("act", lambda e: e.activation(
                            out=fap(KT[0:96, 0, 4 * b * 128:4 * b * 128 + 1], [SEQ, 2], [128, 4], [1, 128]),
                            in_=fap(PT[0:96, 2, 0:1], [128, 2], [512, 4], [1, 128]),
                            func=AF.Copy, scale=gvec[:, 1:2]), r=["PL", "PM", "gvec"], w=["KT"])

                    for b in range(5):
                        if b < 4:
                            mla_s1(b)
                        if b >= 1:
                            mla_s2(b - 1)
                        if b < 4:
                            mla_s1b(b)
                        if b >= 1:
                            mla_s2b(b - 1)
                    conv_step()
                    attention(96, sc_mla, 4 + p)

                norm_flush()
                def D_load(t):
                    em("sp", lambda e: e.dma_start(out=xbuf[t % 2][:], in_=x[tok0 + t * 128: tok0 + (t + 1) * 128, :]),
                       w=["xbuf%d" % (t % 2)], dma="xbuf%d" % (t % 2))

                D_load(0)
                for t in range(NT):
                    xb = xbuf[t % 2]
                    xk = "xbuf%d" % (t % 2)
                    Po = PS[t % 2]
                    pk = "PS%d" % (t % 2)
                    if t >= 1 and t + 1 < NT:
                        D_load(t + 1)
                    for hf in range(2):
                        for c in range(8):
                            em("pe", lambda e, c=c, hf=hf, t=t, Po=Po: e.matmul(
                                Po[:, hf * 512:(hf + 1) * 512], lhsT=OT[:, c, t * 128:(t + 1) * 128],
                                rhs=wo[:, c, hf * 512:(hf + 1) * 512], start=(c == 0), stop=(c == 7)),
                               r=["OT", "hnT"], w=[pk], sig=(c == 7 and hf == 1))
                    em("dve", lambda e, xb=xb, Po=Po: e.tensor_tensor(out=xb[:], in0=Po[:], in1=xb[:], op=ALU.add),
                       r=[pk, xk], w=[xk])
                    em("sp", lambda e, xb=xb, t=t: e.dma_start(out=y[tok0 + t * 128: tok0 + (t + 1) * 128, :], in_=xb[:]),
                       r=[xk], w=["y%d" % (s * NT + t)], dma="st" + xk)
                    if t == 0 and NT > 1:
                        D_load(1)

        while conv_jobs:
            conv_step()
        S.barrier()
        with ExitStack() as es2:
            wg = sb(es2, "wg", [128, 8, DFF], BF16)
            wu = sb(es2, "wu", [128, 8, DFF], BF16)
            wd = sb(es2, "wd", [128, NFF, DM], BF16)
            hbn = [sb(es2, "hbn%d" % i, [128, DM], F32) for i in range(4)]
            hbd = [sb(es2, "hbd%d" % i, [128, DM], F32) for i in range(2)]
            xs2 = [sb(es2, "xsb%d" % i, [128, DM], BF16) for i in range(4)]
            ss2 = sb(es2, "ss2", [128, 8], F32)
            gnT = [sb(es2, "gnT%d" % i, [128, 8, 512], BF16) for i in range(2)]
            hmT = sb(es2, "hmT", [128, NFF, 512], BF16)
            sg = [sb(es2, "sg%d" % i, [128, 512], F32) for i in range(2)]
            print("phase2 sbuf remaining", nc.sbuf_bytes_remaining)
            NBLK = 4
            BW = DFF // NBLK
            wkeys = lambda f: ["wg%d" % j for j in range(NBLK) if f * 128 < (j + 1) * BW and (f + 1) * 128 > j * BW]
            ukeys = lambda f: ["wu%d" % j for j in range(NBLK) if f * 128 < (j + 1) * BW and (f + 1) * 128 > j * BW]
            ngr = ngrp2 if phase2 else 0
            st2 = {"n": 0}

            def norm_pre(grp):
                for tt in range(4):
                    gt = grp * 4 + tt
                    i = st2["n"] % 4
                    st2["n"] += 1
                    hbt, hk = hbn[i], "hbn%d" % i
                    col = ss2[:, (grp % 2) * 4 + tt:(grp % 2) * 4 + tt + 1]
                    ck = "ss2_%d" % ((grp % 2) * 4 + tt)
                    xsb = xs2[tt]
                    xsk = "xsb%d" % tt
                    em("sp", lambda e, hbt=hbt, gt=gt: e.dma_start(out=hbt[:], in_=y[gt * 128:(gt + 1) * 128, :]),
                       r=["y%d" % gt], w=[hk], dma=hk)
                    em("act", lambda e, hbt=hbt, xsb=xsb, col=col: e.activation(out=xsb[:], in_=hbt[:], func=AF.Square,
                                                                               accum_out=col), r=[hk], w=[xsk, ck])
                    rms_rstd(col, col, DM, ck)
                    em("dve", lambda e, hbt=hbt, xsb=xsb, col=col: e.tensor_scalar(out=xsb[:], in0=hbt[:], scalar1=col, scalar2=None,
                                                                                op0=ALU.mult), r=[hk, ck], w=[xsk])

            def norm_stage(grp):
                gb = gnT[grp % 2]
                gk = "gnT%d" % (grp % 2)
                for tt in range(4):
                    xsb = xs2[tt]
                    xsk = "xsb%d" % tt
                    for c in range(8):
                        em("pe", lambda e, c=c, tt=tt, xsb=xsb: e.transpose(
                            out=PSb[c // 4][:, (c % 4) * 512 + tt * 128:(c % 4) * 512 + (tt + 1) * 128],
                            in_=xsb[:, c * 128:(c + 1) * 128], identity=ident[:]),
                           r=[xsk, "ident"], w=["PS%d" % (c // 4)], sig=(c == 7))
                for c in range(8):
                    em("dve", lambda e, c=c: e.tensor_scalar(
                        out=gb[:, c, :], in0=PSb[c // 4][:, (c % 4) * 512:(c % 4 + 1) * 512],
                        scalar1=gF[:, c:c + 1], scalar2=None, op0=ALU.mult),
                       r=["PS%d" % (c // 4), "gF"], w=[gk])

            def gateup_stage(grp):
                gb = gnT[grp % 2]
                gk = "gnT%d" % (grp % 2)
                for f in range(NFF):
                    if grp == 0 and f == 2:
                        load_ffn_weights([2], False)
                    if grp == 0 and f == 7:
                        load_ffn_weights([3], False)
                    if grp == 0 and f == 12:
                        load_ffn_weights([], True)
                    Pg = PS[f % 2][:, 0:512]
                    Pu = PS[f % 2][:, 512:1024]
                    pk = "PS%d" % (f % 2)
                    for c in range(8):
                        em("pe", lambda e, c=c, f=f, Pg=Pg: e.matmul(Pg, lhsT=wg[:, c, f * 128:(f + 1) * 128], rhs=gb[:, c, :],
                                                                    start=(c == 0), stop=(c == 7)),
                           r=wkeys(f) + [gk], w=[pk], sig=False)
                    for c in range(8):
                        em("pe", lambda e, c=c, f=f, Pu=Pu: e.matmul(Pu, lhsT=wu[:, c, f * 128:(f + 1) * 128], rhs=gb[:, c, :],
                                                                    start=(c == 0), stop=(c == 7)),
                           r=ukeys(f) + [gk], w=[pk], sig=(c == 7))
                    sgt = sg[f % 2]
                    sk = "sg%d" % (f % 2)
                    em("act", lambda e, Pg=Pg, sgt=sgt: e.activation(out=sgt[:], in_=Pg, func=AF.Silu), r=[pk], w=[sk])
                    em("dve", lambda e, f=f, Pu=Pu, sgt=sgt: e.tensor_tensor(out=hmT[:, f, :], in0=Pu, in1=sgt[:], op=ALU.mult),
                       r=[pk, sk], w=["hmT"])

            def down_stage(grp):
                for tt in range(4):
                    gt = grp * 4 + tt
                    hbt, hk = hbd[tt % 2], "hbd%d" % (tt % 2)
                    em("sp", lambda e, hbt=hbt, gt=gt: e.dma_start(out=hbt[:], in_=y[gt * 128:(gt + 1) * 128, :]),
                       r=["y%d" % gt], w=[hk], dma=hk)
                    Po = PS[2 + tt % 2]
                    pk = "PQ%d" % (tt % 2)
                    for hf in range(2):
                        for f in range(NFF):
                            em("pe", lambda e, f=f, hf=hf, tt=tt, Po=Po: e.matmul(
                                Po[:, hf * 512:(hf + 1) * 512], lhsT=hmT[:, f, tt * 128:(tt + 1) * 128],
                                rhs=wd[:, f, hf * 512:(hf + 1) * 512], start=(f == 0), stop=(f == NFF - 1)),
                               r=["hmT", "wd"], w=[pk], sig=(f == NFF - 1 and hf == 1))
                    em("dve", lambda e, hbt=hbt, Po=Po: e.tensor_tensor(out=hbt[:], in0=Po[:], in1=hbt[:], op=ALU.add),
                       r=[pk, hk], w=[hk])
                    em("sp", lambda e, hbt=hbt, gt=gt: e.dma_start(out=y[gt * 128:(gt + 1) * 128, :], in_=hbt[:]),
                       r=[hk], w=["y%d" % gt], dma="st" + hk)

            def load_ffn_weights(blocks, with_wd):
                for j in blocks:
                    for c in range(8):
                        em("act", lambda e, c=c, j=j: e.dma_start(out=wg[:, c, j * BW:(j + 1) * BW],
                                                                 in_=wg_bf[c * 128:(c + 1) * 128, j * BW:(j + 1) * BW]),
                           r=["wgbf"], w=["wg%d" % j], dma="wg%d" % j)
                        em("act", lambda e, c=c, j=j: e.dma_start(out=wu[:, c, j * BW:(j + 1) * BW],
                                                                 in_=wu_bf[c * 128:(c + 1) * 128, j * BW:(j + 1) * BW]),
                           r=["wubf"], w=["wu%d" % j], dma="wu%d" % j)
                if with_wd:
                    em("act", lambda e: e.dma_start(out=wd[:], in_=wd_bf.rearrange("(f p) n -> p f n", p=128)),
                       r=["wdbf"], w=["wd"], dma="wd")

            if ngr:
                norm_pre(0)
            load_ffn_weights([0, 1], False)
            if ngr:
                norm_stage(0)
            for grp in range(ngr):
                if grp + 1 < ngr:
                    norm_pre(grp + 1)
                gateup_stage(grp)
                if grp + 1 < ngr:
                    norm_stage(grp + 1)
                down_stage(grp)
            for key, sm in S.dsem.items():
                if key.startswith("st"):
                    nc.sync.wait_ge(sm, S.dcnt[key])
    return nc


def _consts():
    def tables(dim):
        inv = (10000.0 ** (-np.arange(0, dim, 2, dtype=np.float32) / np.float32(dim))).astype(np.float32)
        ang = (np.arange(SEQ, dtype=np.float32)[:, None] * inv[None, :]).astype(np.float32)
        return np.cos(ang).astype(np.float32), np.sin(ang).astype(np.float32)

    cA, sA = tables(64)
    cB, sB = tables(32)
    blk = (np.arange(SEQ)[None, :] // 256 == np.arange(8)[:, None]).astype(np.float32)
    return {"cosA_tab": cA, "sinA_tab": sA, "cosB_tab": cB, "sinB_tab": sB, "blkind": blk}


_NC = None


def kernel(**inputs):
    global _NC
    if _NC is None:
        _NC = build()
    nc = _NC
    x = np.ascontiguousarray(np.asarray(inputs["x"], dtype=np.float32)).reshape(16 * SEQ, DM)
    common = {k: np.ascontiguousarray(np.asarray(v, dtype=np.float32)[0]) for k, v in inputs.items() if k != "x"}
    common.update(_consts())
    in_maps = []
    for c in range(NCORES):
        m = dict(common)
        m["x"] = x[c * TOK:(c + 1) * TOK]
        in_maps.append(m)
    res = run_bass_kernel_spmd(nc, in_maps, core_ids=list(range(NCORES)))
    out = np.concatenate([r["y"] for r in res.results], axis=0)
    return out.reshape(16, SEQ, DM).astype(np.float32)
```
